# Optimizing a Trainium2 kernel written in Bass

```python
import jax
import jax.numpy as jnp
from jax import lax
import numpy as np

D_MODEL = 1024
BATCH = 4
SEQ = 8192
DEPTH = 2

N_GROUPS = 4
GROUP_WIDTH = D_MODEL // N_GROUPS
D_MIX = N_GROUPS * GROUP_WIDTH
HEAD_DIM = 64
CHUNK = 64
EPS = 1e-6
MASK_VALUE = -1e30
MIN_POS = 1e-30

HA = GROUP_WIDTH // HEAD_DIM
DK_A = HEAD_DIM
DV_A = HEAD_DIM
HB = GROUP_WIDTH // HEAD_DIM
DK_B = HEAD_DIM // 2
DV_B = HEAD_DIM
GLA_GATE_RANK = 16
GLA_GATE_NORMALIZER = 16.0
HC = GROUP_WIDTH // HEAD_DIM
MLA_Q_RANK = 256
MLA_KV_RANK = 128
MLA_NOPE = 64
MLA_ROPE = 32
MLA_V = HEAD_DIM
ROPE_THETA = 10000.0
Q_BLOCK = 128
HD = GROUP_WIDTH // HEAD_DIM
DK_D = HEAD_DIM
DV_D = HEAD_DIM
GDN_CONV = 4
D_FF = 2816
FFN_CONV = 3

IN_SPLITS = (
    HA * DK_A, HA * DK_A, HA * DV_A, HA * DV_A,
    HB * DK_B, HB * DK_B, HB * DV_B, GLA_GATE_RANK, HB * DV_B,
    MLA_Q_RANK, MLA_KV_RANK, MLA_ROPE,
    HD * DK_D, HD * DK_D, HD * DV_D, HD, HD, HD * DV_D,
)
D_IN = sum(IN_SPLITS)

kernel_name = 'hybrid_parallel_head_groups_block'


def rms_norm(x, g):
    xf = x.astype(jnp.float32)
    y = xf * lax.rsqrt(jnp.mean(xf * xf, axis=-1, keepdims=True) + EPS)
    return (y * g).astype(x.dtype)


def l2_norm(x):
    xf = x.astype(jnp.float32)
    return xf * lax.rsqrt(jnp.sum(xf * xf, axis=-1, keepdims=True) + EPS)


def heads(t, h, d):
    return t.reshape(t.shape[:-1] + (h, d))


def split_columns(z):
    out = []
    start = 0
    for w in IN_SPLITS:
        out.append(z[..., start:start + w])
        start += w
    return out


def causal_dwconv(x, w):
    k, c = w.shape
    return lax.conv_general_dilated(
        x, w[:, None, :].astype(x.dtype), window_strides=(1,), padding=[(k - 1, 0)],
        dimension_numbers=('NWC', 'WIO', 'NWC'), feature_group_count=c)


def rope_tables(s):
    inv = ROPE_THETA ** (-jnp.arange(0, MLA_ROPE, 2, dtype=jnp.float32) / MLA_ROPE)
    ang = jnp.arange(s, dtype=jnp.float32)[:, None] * inv[None, :]
    return jnp.cos(ang), jnp.sin(ang)


def apply_rope(x, cos, sin):
    xf = x.astype(jnp.float32)
    x1, x2 = xf[..., :MLA_ROPE // 2], xf[..., MLA_ROPE // 2:]
    return jnp.concatenate([x1 * cos - x2 * sin, x2 * cos + x1 * sin], axis=-1).astype(x.dtype)


def chunk_gla(q, k, v, log_f):
    bsz, s, h, dk = q.shape
    dv = v.shape[-1]
    n = s // CHUNK

    def to_chunks(t):
        return t.astype(jnp.float32).reshape(bsz, n, CHUNK, h, t.shape[-1]).transpose(1, 0, 3, 2, 4)

    qc, kc, vc, gc = to_chunks(q), to_chunks(k), to_chunks(v), to_chunks(log_f)
    bc = jnp.cumsum(gc, axis=3)
    causal = jnp.tril(jnp.ones((CHUNK, CHUNK), bool))[:, :, None]

    def step(state, inp):
        q_, k_, v_, b_ = inp
        diff = b_[:, :, :, None, :] - b_[:, :, None, :, :]
        decay = jnp.where(causal, jnp.exp(jnp.where(causal, diff, 0.0)), 0.0)
        attn = jnp.einsum('bhid,bhjd,bhijd->bhij', q_, k_, decay)
        b_last = b_[:, :, -1, :]
        o = (jnp.einsum('bhij,bhjv->bhiv', attn, v_)
             + jnp.einsum('bhid,bhdv->bhiv', q_ * jnp.exp(b_), state))
        state = (jnp.exp(b_last)[..., None] * state
                 + jnp.einsum('bhjd,bhjv->bhdv', k_ * jnp.exp(b_last[:, :, None, :] - b_), v_))
        return state, o

    s0 = jnp.zeros((bsz, h, dk, dv), jnp.float32)
    _, o = lax.scan(step, s0, (qc, kc, vc, bc))
    return o.transpose(1, 0, 3, 2, 4).reshape(bsz, s, h, dv)


def chunk_gated_delta(q, k, v, beta, log_g):
    bsz, s, h, dk = q.shape
    dv = v.shape[-1]
    n = s // CHUNK

    def to_chunks(t):
        t = t.astype(jnp.float32).reshape((bsz, n, CHUNK, h) + t.shape[3:])
        return jnp.moveaxis(t, (1, 3), (0, 2))

    qc, kc, vc = to_chunks(q), to_chunks(k), to_chunks(v)
    bt, gc = to_chunks(beta), to_chunks(log_g)
    b = jnp.cumsum(gc, axis=-1)
    incl = jnp.tril(jnp.ones((CHUNK, CHUNK), bool))
    strict = jnp.tril(jnp.ones((CHUNK, CHUNK), bool), -1)
    diff = b[..., :, None] - b[..., None, :]
    lmask = jnp.where(incl, jnp.exp(jnp.where(incl, diff, 0.0)), 0.0)
    kb = kc * bt[..., None]
    m = jnp.where(strict, jnp.einsum('nbhid,nbhjd->nbhij', kb, kc) * lmask, 0.0)
    eye = jnp.eye(CHUNK, dtype=jnp.float32)
    rhs = jnp.concatenate([vc * bt[..., None], kb * jnp.exp(b)[..., None]], axis=-1)
    sol = lax.linalg.triangular_solve(m + eye, rhs, left_side=True, lower=True, unit_diagonal=True)
    u, w = sol[..., :dv], sol[..., dv:]
    a_qk = jnp.einsum('nbhid,nbhjd->nbhij', qc, kc) * lmask

    def step(state, inp):
        q_, k_, u_, w_, a_, b_ = inp
        v_new = u_ - jnp.einsum('bhcd,bhdv->bhcv', w_, state)
        o = (jnp.einsum('bhcd,bhdv->bhcv', q_ * jnp.exp(b_)[..., None], state)
             + jnp.einsum('bhij,bhjv->bhiv', a_, v_new))
        b_last = b_[..., -1:]
        state = (jnp.exp(b_last)[..., None] * state
                 + jnp.einsum('bhcd,bhcv->bhdv', k_ * jnp.exp(b_last - b_)[..., None], v_new))
        return state, o

    s0 = jnp.zeros((bsz, h, dk, dv), jnp.float32)
    _, o = lax.scan(step, s0, (qc, kc, u, w, a_qk, b))
    return jnp.moveaxis(o, (0, 2), (1, 3)).reshape(bsz, s, h, dv)


def mla_attention(q_nope, q_rope, k_nope, k_rope, v):
    bsz, s, h, _ = q_nope.shape
    nb = s // Q_BLOCK
    scale = (MLA_NOPE + MLA_ROPE) ** -0.5
    key_pos = jnp.arange(s)

    def blocks(t):
        return jnp.moveaxis(t.reshape((bsz, nb, Q_BLOCK) + t.shape[2:]), 1, 0)

    def attend(args):
        qn, qr, blk = args
        sc = (jnp.einsum('bqhd,bkhd->bhqk', qn, k_nope)
              + jnp.einsum('bqhd,bkd->bhqk', qr, k_rope)).astype(jnp.float32) * scale
        q_pos = blk * Q_BLOCK + jnp.arange(Q_BLOCK)
        mask = key_pos[None, :] <= q_pos[:, None]
        p = jax.nn.softmax(jnp.where(mask, sc, MASK_VALUE), axis=-1)
        return jnp.einsum('bhqk,bkhv->bqhv', p.astype(v.dtype), v)

    o = lax.map(attend, (blocks(q_nope), blocks(q_rope), jnp.arange(nb)))
    return jnp.moveaxis(o, 0, 1).reshape(bsz, s, h, v.shape[-1])


def hgrn_lower_bounds(lb_logits):
    p = jax.nn.softmax(lb_logits.astype(jnp.float32), axis=0)
    return jnp.cumsum(p, axis=0) - p[0]


def hgrn2_mixer(q, f_logit, i, g, lb, norm_g):
    dtype = q.dtype
    zf = f_logit.astype(jnp.float32)
    f = lb + (1.0 - lb) * jax.nn.sigmoid(zf)
    log_f = jnp.log(jnp.maximum(f, MIN_POS))
    k = (1.0 - lb) * jax.nn.sigmoid(-zf)
    qh = heads(jax.nn.silu(q), HA, DK_A) * DK_A ** -0.5
    o = chunk_gla(qh, heads(k, HA, DK_A), heads(i, HA, DV_A), heads(log_f, HA, DK_A))
    o = rms_norm(o, norm_g) * jax.nn.sigmoid(heads(g, HA, DV_A).astype(jnp.float32))
    return o.reshape(o.shape[:2] + (HA * DV_A,)).astype(dtype)


def gla_mixer(q, k, v, gate_code, g, w_gk2, b_gk, norm_g):
    dtype = q.dtype
    log_gk = jax.nn.log_sigmoid((gate_code @ w_gk2 + b_gk).astype(jnp.float32)) / GLA_GATE_NORMALIZER
    o = chunk_gla(heads(q, HB, DK_B) * DK_B ** -0.5, heads(k, HB, DK_B), heads(v, HB, DV_B),
                  heads(log_gk, HB, DK_B))
    o = rms_norm(o, norm_g) * jax.nn.silu(heads(g, HB, DV_B).astype(jnp.float32))
    return o.reshape(o.shape[:2] + (HB * DV_B,)).astype(dtype)


def mla_mixer(c_q, c_kv, k_rope, q_norm_g, w_uq, kv_norm_g, w_ukv, cos, sin):
    dtype = c_q.dtype
    q = heads(rms_norm(c_q, q_norm_g) @ w_uq, HC, MLA_NOPE + MLA_ROPE)
    kv = heads(rms_norm(c_kv, kv_norm_g) @ w_ukv, HC, MLA_NOPE + MLA_V)
    q_nope = q[..., :MLA_NOPE]
    q_rope = apply_rope(q[..., MLA_NOPE:], cos[:, None, :], sin[:, None, :])
    k_nope, v = kv[..., :MLA_NOPE], kv[..., MLA_NOPE:]
    k_rope = apply_rope(k_rope, cos, sin)
    o = mla_attention(q_nope, q_rope, k_nope, k_rope, v)
    return o.reshape(o.shape[:2] + (HC * MLA_V,)).astype(dtype)


def gdn_mixer(q, k, v, beta_logit, a, z, conv_w, a_log, dt_bias, norm_g):
    dtype = q.dtype
    qkv = jax.nn.silu(causal_dwconv(jnp.concatenate([q, k, v], axis=-1), conv_w))
    qd = l2_norm(heads(qkv[..., :HD * DK_D], HD, DK_D)) * DK_D ** -0.5
    kd = l2_norm(heads(qkv[..., HD * DK_D:2 * HD * DK_D], HD, DK_D))
    vd = heads(qkv[..., 2 * HD * DK_D:], HD, DV_D)
    beta = jax.nn.sigmoid(beta_logit.astype(jnp.float32))
    log_g = -jnp.exp(a_log.astype(jnp.float32)) * jax.nn.softplus(a.astype(jnp.float32) + dt_bias)
    o = chunk_gated_delta(qd, kd, vd, beta, log_g)
    o = rms_norm(o, norm_g) * jax.nn.silu(heads(z, HD, DV_D).astype(jnp.float32))
    return o.reshape(o.shape[:2] + (HD * DV_D,)).astype(dtype)


def conv_glu_ffn(h, w_gate, w_up, conv_w, w_down):
    gate = causal_dwconv(h @ w_gate, conv_w)
    return (jax.nn.gelu(gate, approximate=True) * (h @ w_up)) @ w_down


def setup_inputs(seed: int = 0) -> dict:
    key = jax.random.key(seed)
    ks = jax.random.split(key, 26)

    def nrm(k, shape, scale):
        return jax.random.normal(k, shape, jnp.float32) * scale

    def gain(k, shape):
        return 1.0 + 0.1 * jax.random.normal(k, shape, jnp.float32)

    dt = jnp.exp(jax.random.uniform(ks[19], (DEPTH, HD), jnp.float32,
                                    jnp.log(1e-3), jnp.log(1e-1)))
    return {
        'x': jax.random.normal(ks[0], (BATCH, SEQ, D_MODEL), jnp.float32),
        'w_in': nrm(ks[1], (DEPTH, D_MODEL, D_IN), D_MODEL ** -0.5),
        'w_out': nrm(ks[2], (DEPTH, D_MIX, D_MODEL), D_MIX ** -0.5),
        'pre_mix_g': gain(ks[3], (DEPTH, D_MODEL)),
        'post_mix_g': gain(ks[4], (DEPTH, D_MODEL)),
        'pre_ffn_g': gain(ks[5], (DEPTH, D_MODEL)),
        'post_ffn_g': gain(ks[6], (DEPTH, D_MODEL)),
        'hgrn_lb_logits': nrm(ks[7], (DEPTH, HA * DK_A), 1.0),
        'hgrn_norm_g': gain(ks[8], (DEPTH, DV_A)),
        'gla_w_gk2': nrm(ks[9], (DEPTH, GLA_GATE_RANK, HB * DK_B), GLA_GATE_RANK ** -0.5),
        'gla_b_gk': nrm(ks[10], (DEPTH, HB * DK_B), 0.1),
        'gla_norm_g': gain(ks[11], (DEPTH, DV_B)),
        'mla_q_norm_g': gain(ks[12], (DEPTH, MLA_Q_RANK)),
        'mla_w_uq': nrm(ks[13], (DEPTH, MLA_Q_RANK, HC * (MLA_NOPE + MLA_ROPE)), MLA_Q_RANK ** -0.5),
        'mla_kv_norm_g': gain(ks[14], (DEPTH, MLA_KV_RANK)),
        'mla_w_ukv': nrm(ks[15], (DEPTH, MLA_KV_RANK, HC * (MLA_NOPE + MLA_V)), MLA_KV_RANK ** -0.5),
        'gdn_conv_w': nrm(ks[16], (DEPTH, GDN_CONV, HD * (2 * DK_D + DV_D)), GDN_CONV ** -0.5),
        'gdn_a_log': jnp.log(jax.random.uniform(ks[17], (DEPTH, HD), jnp.float32, 1.0, 16.0)),
        'gdn_dt_bias': dt + jnp.log(-jnp.expm1(-dt)),
        'gdn_norm_g': gain(ks[18], (DEPTH, DV_D)),
        'ffn_w_gate': nrm(ks[20], (DEPTH, D_MODEL, D_FF), D_MODEL ** -0.5),
        'ffn_w_up': nrm(ks[21], (DEPTH, D_MODEL, D_FF), D_MODEL ** -0.5),
        'ffn_conv_w': nrm(ks[22], (DEPTH, FFN_CONV, D_FF), FFN_CONV ** -0.5),
        'ffn_w_down': nrm(ks[23], (DEPTH, D_FF, D_MODEL), D_FF ** -0.5),
    }


def reference(x, w_in, w_out, pre_mix_g, post_mix_g, pre_ffn_g, post_ffn_g,
              hgrn_lb_logits, hgrn_norm_g, gla_w_gk2, gla_b_gk, gla_norm_g,
              mla_q_norm_g, mla_w_uq, mla_kv_norm_g, mla_w_ukv,
              gdn_conv_w, gdn_a_log, gdn_dt_bias, gdn_norm_g,
              ffn_w_gate, ffn_w_up, ffn_conv_w, ffn_w_down):
    s = x.shape[1]
    cos, sin = rope_tables(s)
    lower_bounds = hgrn_lower_bounds(hgrn_lb_logits)
    for l in range(DEPTH):
        h = rms_norm(x, pre_mix_g[l])
        z = h @ w_in[l]
        (a_q, a_f, a_i, a_g,
         b_q, b_k, b_v, b_code, b_g,
         c_q, c_kv, c_kr,
         d_q, d_k, d_v, d_beta, d_a, d_z) = split_columns(z)
        o_a = hgrn2_mixer(a_q, a_f, a_i, a_g, lower_bounds[l], hgrn_norm_g[l])
        o_b = gla_mixer(b_q, b_k, b_v, b_code, b_g, gla_w_gk2[l], gla_b_gk[l], gla_norm_g[l])
        o_c = mla_mixer(c_q, c_kv, c_kr, mla_q_norm_g[l], mla_w_uq[l], mla_kv_norm_g[l], mla_w_ukv[l], cos, sin)
        o_d = gdn_mixer(d_q, d_k, d_v, d_beta, d_a, d_z, gdn_conv_w[l], gdn_a_log[l], gdn_dt_bias[l], gdn_norm_g[l])
        mix = jnp.concatenate([o_a, o_b, o_c, o_d], axis=-1).astype(h.dtype) @ w_out[l]
        x = x + rms_norm(mix, post_mix_g[l])
        h = rms_norm(x, pre_ffn_g[l])
        y = conv_glu_ffn(h, ffn_w_gate[l], ffn_w_up[l], ffn_conv_w[l], ffn_w_down[l])
        x = x + rms_norm(y, post_ffn_g[l])
    return x
```

```python
import numpy as np
from contextlib import ExitStack
import concourse.bass as bass
import concourse.mybir as mybir
from concourse.bass_utils import run_bass_kernel_spmd

F32 = mybir.dt.float32
BF16 = mybir.dt.bfloat16
AF = mybir.ActivationFunctionType
ALU = mybir.AluOpType

D_MODEL = 1024
D_IN = 3256
D_FF = 2816
NFF = D_FF // 128
TT = 256
NCH = TT // 64
NBK = TT // 128
EPS = 1e-6
NPP = 166


class Sched:
    ENGS = ("pe", "act", "dve", "pool", "sp")

    def __init__(self, nc):
        self.nc = nc
        self.ops = {e: [] for e in self.ENGS}
        self.last_w = {}
        self.readers = {}
        self.dma_cnt = {}
        self.seen = {e: {} for e in self.ENGS}
        self.ecnt = {}

    def _need(self, eng, tok, waits, raw):
        if tok is None:
            return
        sem, val, teng, tidx = tok
        if teng == eng:
            if eng == "pe" or not raw:
                return
            if tidx != len(self.ops[eng]) - 1:
                return
        if self.seen[eng].get(sem, 0) >= val:
            return
        self.seen[eng][sem] = val
        waits.append((sem, val))

    def add(self, eng, fn, reads=(), writes=(), dma_sem=None):
        waits = []
        for k in reads:
            self._need(eng, self.last_w.get(k), waits, True)
        for k in writes:
            self._need(eng, self.last_w.get(k), waits, False)
            for r in self.readers.get(k, {}).values():
                self._need(eng, r, waits, False)
        idx = len(self.ops[eng])
        if dma_sem is None:
            self.ecnt[eng] = self.ecnt.get(eng, 0) + 1
            tok = ("e_" + eng, self.ecnt[eng], eng, idx)
            rk = eng
        else:
            c = self.dma_cnt.get(dma_sem, 0) + 1
            self.dma_cnt[dma_sem] = c
            tok = (dma_sem, 16 * c, None, idx)
            rk = dma_sem
        self.ops[eng].append((fn, waits, dma_sem))
        for k in reads:
            self.readers.setdefault(k, {})[rk] = tok
        for k in writes:
            self.last_w[k] = tok
            self.readers[k] = {}
        return tok

    def emit(self, final_reads=()):
        nc = self.nc
        for e in ("sp", "pool"):
            self.add(e, lambda en: en.nop(), reads=final_reads)
        with ExitStack() as st:
            sems = {}
            for e in self.ENGS:
                sems["e_" + e] = st.enter_context(nc.semaphore("e_" + e))
            for name in self.dma_cnt:
                sems[name] = st.enter_context(nc.semaphore(name))
            with nc.Block() as block:
                def mk(engname):
                    def run(e):
                        for fn, waits, dma_sem in self.ops[engname]:
                            for s, v in waits:
                                e.wait_ge(sems[s], v)
                            ins = fn(e)
                            if dma_sem is None:
                                ins.then_inc(sems["e_" + engname], 1)
                            else:
                                ins.then_inc(sems[dma_sem], 16)
                    return run
                block.tensor(mk("pe"))
                block.scalar(mk("act"))
                block.vector(mk("dve"))
                block.gpsimd(mk("pool"))
                block.sync(mk("sp"))


class _Stop(Exception):
    pass


def build(S, DEPTH, stop=None):
    NT = S // TT
    NKB = S // 128
    nc = bass.Bass("TRN2", target_bir_lowering=False)
    dI = lambda n, s: nc.dram_tensor(n, s, F32, kind="ExternalInput").ap()
    xT = dI("xT", [D_MODEL, S])
    w_in = dI("w_in", [DEPTH, D_MODEL, D_IN])
    w_out = dI("w_out", [DEPTH, 1024, 1024])
    w_gate = dI("ffn_w_gate", [DEPTH, 1024, D_FF])
    w_up = dI("ffn_w_up", [DEPTH, 1024, D_FF])
    w_down = dI("ffn_w_down", [DEPTH, D_FF, 1024])
    w_uq = dI("mla_w_uq", [DEPTH, 256, 384])
    w_ukv = dI("mla_w_ukv", [DEPTH, 128, 512])
    w_gk2 = dI("gla_w_gk2", [DEPTH, 16, 128])
    pp_d = dI("pp", [128, DEPTH, NPP])
    tabq = dI("tabq", [96, 2, S])
    tabk = dI("tabk", [32, 2, S])
    outT = nc.dram_tensor("outT", [D_MODEL, S], F32, kind="ExternalOutput").ap()
    xs = [nc.dram_tensor(f"xs{i}", [D_MODEL, S], F32, kind="Internal").ap() for i in range(2)]
    kd = [nc.dram_tensor(f"kd{h}", [96, S], BF16, kind="Internal").ap() for h in range(4)]

    pieces = {}
    plist = []
    for l in range(DEPTH):
        wi = w_in[l].rearrange("(c p) n -> p c n", p=128)
        for nm, lo, hi in (("A0", 0, 512), ("A1", 512, 1024), ("B0", 1024, 1536), ("B1", 1536, 1808), ("C", 1808, 2224),
                           ("D0", 2224, 2736), ("D1", 2736, 2992), ("D2", 2992, 3256)):
            plist.append((f"{nm}_{l}", wi[:, :, lo:hi], 128, 8, hi - lo))
        wo = w_out[l].rearrange("(c p) n -> p c n", p=64)
        for j in range(4):
            plist.append((f"O{j}_{l}", wo[:, :, j * 256:(j + 1) * 256], 64, 16, 256))
        wg = w_gate[l].rearrange("(c p) n -> p c n", p=128)
        wu = w_up[l].rearrange("(c p) n -> p c n", p=128)
        for j in range(6):
            hi = min(D_FF, (j + 1) * 512)
            plist.append((f"G{j}_{l}", wg[:, :, j * 512:hi], 128, 8, hi - j * 512))
            plist.append((f"U{j}_{l}", wu[:, :, j * 512:hi], 128, 8, hi - j * 512))
        wd = w_down[l].rearrange("(c p) n -> p c n", p=128)
        for j in range(8):
            plist.append((f"W{j}_{l}", wd[:, :, j * 128:(j + 1) * 128], 128, NFF, 128))
    for i, (nm, src, P, kc, cols) in enumerate(plist):
        pieces[nm] = (i, P, kc, cols)
    wsc = nc.dram_tensor("wsc", [len(plist), 128, 4096], BF16, kind="Internal").ap()

    Sd = Sched(nc)
    BANK = {"pd0": 0, "pd1": 1, "pq0": 6, "pq1": 0, "pq2": 1, "pq3": 5, "pss": 5, "psa0": 2, "psa1": 3, "po": 4,
            "pss2": 4, "ptb0": 7, "ptb1": 7}

    def add(eng, fn, reads=(), writes=(), dma_sem=None):
        reads, writes = list(reads), list(writes)
        if eng == "pe":
            writes += [f"BK{BANK[k]}" for k in writes if k in BANK]
        elif eng == "dve":
            reads += [f"BK{BANK[k]}" for k in reads if k in BANK]
        return Sd.add(eng, fn, reads, writes, dma_sem)

    with ExitStack() as st:
        def sb(n, s, d=F32):
            return st.enter_context(nc.sbuf_tensor(n, s, d))

        def psf(n):
            return st.enter_context(nc.psum_tensor(n, [128, 256], F32))

        xt = sb("xt", [128, 8, TT])
        hb = sb("hb", [128, 8, TT], BF16)
        yb = sb("yb", [128, 8, TT])
        sqb = sb("sqb", [128, TT], BF16)
        rstd = sb("rstd", [128, TT])
        NWB = 4
        wbuf = [sb(f"wbuf{j}", [128, 4096], BF16) for j in range(NWB)]
        stg32 = [sb(f"stg32_{j}", [128, 2048]) for j in range(2)]
        pp = sb("pp_sb", [128, DEPTH, NPP])
        ones = sb("ones", [128, 128], BF16)
        ident = sb("ident", [128, 128], BF16)
        identf = sb("identf", [128, 128])
        seg = sb("seg", [128, TT])
        mU = sb("mU", [64, NCH, 64])
        mUs = sb("mUs", [64, NCH, 64])
        mLs = sb("mLs", [64, NCH, 64])
        tri = sb("tri", [128, 128], BF16)
        sel = sb("sel", [4, 4, 64])
        Esel = sb("Esel", [65, 64])
        rotq = sb("rotq", [96, 96], BF16)
        rotk = sb("rotk", [32, 32], BF16)
        padI = sb("padI", [32, 96], BF16)
        mix = sb("mix", [64, 16, TT], BF16)
        hid = sb("hid", [128, NFF, TT], BF16)
        gtail = sb("gtail", [128, NFF, 2])
        graw = sb("graw", [128, TT + 2])
        NTMP = 8
        tf = [sb(f"tf{i}", [128, TT]) for i in range(NTMP)]
        tb = [sb(f"tb{i}", [128, TT], BF16) for i in range(NTMP)]
        Sa32 = [sb(f"Sa32_{h}", [64, 64]) for h in range(4)]
        Sb32 = [sb(f"Sb32_{h}", [32, 64]) for h in range(4)]
        Sd32 = [sb(f"Sd32_{h}", [64, 64]) for h in range(4)]
        Sbf = sb("Sbf", [64, NCH, 64], BF16)
        Sdbf = [sb(f"Sdbf_{h}", [64, 64], BF16) for h in range(4)]
        vtok = sb("vtok", [64, NCH, 256], BF16)
        k2tok = sb("k2tok", [64, NCH, 64], BF16)
        attT = sb("attT", [64, NCH, 64], BF16)
        acol = sb("acol", [64, NCH])
        blast = sb("blast", [64, NCH])
        codeb = sb("codeb", [16, TT], BF16)
        wgk = sb("wgk", [16, 128], BF16)
        wuq = sb("wuq", [128, 2, 384], BF16)
        wkpad = sb("wkpad", [128, 4, 96], BF16)
        wv = sb("wv", [128, 4, 64], BF16)
        cqf = sb("cqf", [128, 2, TT])
        cqn = sb("cqn", [128, 2, TT], BF16)
        ckvn = sb("ckvn", [128, TT], BF16)
        krope = sb("krope", [32, TT], BF16)
        qTh = [sb(f"qTh{h}", [96, TT], BF16) for h in range(4)]
        ksb = [sb(f"ksb{i}", [96, TT], BF16) for i in range(2)]
        KP = 1024
        kbuf = [sb(f"kbuf{i}", [96, KP], BF16) for i in range(2)]
        vc = sb("vc", [128, NKB, 260], BF16)
        tq = sb("tq", [96, 2, TT])
        tk = sb("tk", [32, 2, TT])
        pT = [sb(f"pT{i}", [128, TT], BF16) for i in range(2)]
        ob = sb("ob", [65, TT])
        tfa = sb("tfa", [64, TT])
        graw2 = sb("graw2", [128, TT + 2])
        ctail = sb("ctail", [64, 12, 3])
        cbuf = sb("cbuf", [64, TT + 3])
        dq = [sb(f"dq{i}", [64, TT]) for i in range(3)]
        bet4 = sb("bet4", [4, TT])
        lg4 = sb("lg4", [4, TT])
        b4 = sb("b4", [4, TT])
        negA = sb("negA", [4, DEPTH])
        btok = sb("btok", [64, NCH, 4])
        lmT = sb("lmT", [64, NCH, 64])
        lm = sb("lm", [64, NCH, 64])
        Zb = [sb(f"Zb{i}", [64, NCH, 64], BF16) for i in range(2)]
        Yb = [sb(f"Yb{i}", [64, NCH, 64], BF16) for i in range(2)]
        P32 = sb("P32", [64, NCH, 64])
        Pb = sb("Pb", [64, NCH, 64], BF16)
        aqk = sb("aqk", [64, NCH, 64], BF16)
        kbf = sb("kbf", [64, TT], BF16)
        kbb = sb("kbb", [64, TT], BF16)
        qbf = sb("qbf", [64, TT], BF16)
        q2b = sb("q2b", [64, TT], BF16)
        vbtok = sb("vbtok", [64, NCH, 64], BF16)
        kbetok = sb("kbetok", [64, NCH, 64], BF16)
        ud = sb("ud", [64, NCH, 64])
        wTd = sb("wTd", [64, NCH, 64], BF16)
        vnew = sb("vnew", [64, 64], BF16)
        elast = sb("elast", [64, NCH])
        lbt = sb("lbt", [64, 4])
        oml = sb("oml", [64, 4])
        noml = sb("noml", [64, 4])

        banks = [st.enter_context(nc.psum_tensor(f"bank{i}", [128, 512], F32)) for i in range(7)]
        h0 = lambda i: banks[i][:, 0:256]
        h1 = lambda i: banks[i][:, 256:512]
        bankb = st.enter_context(nc.psum_tensor("bankb", [128, 1024], BF16))
        ptb = [bankb[:, 0:512], bankb[:, 512:1024]]
        pd = [h0(0), h0(1)]
        pq = [h0(6), h1(0), h1(1), h1(5)]
        pss = h0(5)
        psa = [h0(2), h0(3)]
        po = h0(4)
        pss2 = h1(4)

        def mm(out, lhsT, rhs, r, w, start=True, stop=True):
            add("pe", lambda e: e.matmul(out, lhsT=lhsT, rhs=rhs, start=start, stop=stop), r, w)

        def tr(out, in_, idn, r, w):
            add("pe", lambda e: e.transpose(out, in_, idn), r, w)

        def act(out, in_, func, r, w, bias=None, scale=None):
            kw = {}
            if bias is not None:
                kw["bias"] = bias
            if scale is not None:
                kw["scale"] = scale
            add("act", lambda e: e.activation(out=out, in_=in_, func=func, **kw), r, w)

        def tt(eng, out, a, b, op, r, w):
            add(eng, lambda e: e.tensor_tensor(out=out, in0=a, in1=b, op=op), r, w)

        def ts(eng, out, a, s1, op0, r, w, s2=None, op1=None):
            if op1 is None:
                add(eng, lambda e: e.tensor_scalar(out=out, in0=a, scalar1=s1, scalar2=None, op0=op0), r, w)
            else:
                add(eng, lambda e: e.tensor_scalar(out=out, in0=a, scalar1=s1, scalar2=s2, op0=op0, op1=op1), r, w)

        def stt(eng, out, a, s, b, op0, op1, r, w):
            add(eng, lambda e: e.scalar_tensor_tensor(out=out, in0=a, scalar=s, in1=b, op0=op0, op1=op1), r, w)

        def cp(eng, out, in_, r, w):
            if eng == "act":
                act(out, in_, AF.Copy, r, w)
            else:
                add(eng, lambda e: e.tensor_copy(out=out, in_=in_), r, w)

        def ms(eng, ap, v, w):
            add(eng, lambda e: e.memset(ap, v), (), w)

        def asel(out, pattern, op, base, cm, key):
            add("pool", lambda e: e.affine_select(out=out, in_=out, pattern=pattern, compare_op=op, fill=0.0,
                                                  base=base, channel_multiplier=cm), [key], [key])

        def dma(eng, out, in_, r, w, sem):
            add(eng, lambda e: e.dma_start(out=out, in_=in_), r, w, dma_sem=sem)

        def rs_from_ss(ps_ap, n_feat, P):
            act(rstd[:P, :], ps_ap, AF.Sqrt, ["pss"], ["rstd"], bias=EPS, scale=1.0 / n_feat)
            add("dve", lambda e: e.reciprocal(out=rstd[:P, :], in_=rstd[:P, :]), ["rstd"], ["rstd"])

        def ck(name, l, n, tsl, dst, dkey):
            if stop == name:
                dma("pool", dst.rearrange("(c p) t -> p c t", p=128)[:, :, tsl], xt[:], ["xt"], [(dkey, n)], "dout")
                raise _Stop()
        try:
          if stop == "const0":
              ms("pool", xt[:], 0.0, ["xt"])
          ck("const0", 0, 0, slice(0, TT), outT, "outT")
          ms("pool", ones[:], 1.0, ["ones"])
          ms("pool", ident[:], 1.0, ["ident"])
          asel(ident[:], [[-1, 128]], ALU.is_equal, 0, 1, "ident")
          ms("pool", identf[:], 1.0, ["identf"])
          asel(identf[:], [[-1, 128]], ALU.is_equal, 0, 1, "identf")
          ms("pool", seg[:], 1.0, ["seg"])
          ms("pool", seg[:].rearrange("p (c t) -> p c t", t=64)[:, :, 0:1], 0.0, ["seg"])
          if stop == "const1":
              ms("pool", xt[:], 0.0, ["xt"])
          ck("const1", 0, 0, slice(0, TT), outT, "outT")
          ms("pool", mU[:], 1.0, ["mU"])
          asel(mU[:], [[0, NCH], [1, 64]], ALU.is_ge, 0, -1, "mU")
          ms("pool", mUs[:], -1.0, ["mUs"])
          asel(mUs[:], [[0, NCH], [1, 64]], ALU.is_gt, 0, -1, "mUs")
          ms("pool", mLs[:], -1.0, ["mLs"])
          asel(mLs[:], [[0, NCH], [-1, 64]], ALU.is_gt, 0, 1, "mLs")
          ms("pool", tri[:], 1.0, ["tri"])
          asel(tri[:], [[1, 128]], ALU.is_ge, 0, -1, "tri")
          ms("pool", sel[:], 1.0, ["sel"])
          asel(sel[:], [[-1, 4], [0, 64]], ALU.is_equal, 0, 1, "sel")
          ms("pool", Esel[:], 1.0, ["Esel"])
          asel(Esel[:], [[0, 64]], ALU.is_ge, -64, 1, "Esel")
          if stop == "const2":
              ms("pool", xt[:], 0.0, ["xt"])
          ck("const2", 0, 0, slice(0, TT), outT, "outT")
          for nm, t_, n, r0 in (("rotq", rotq, 96, 64), ("rotk", rotk, 32, 0)):
              ms("pool", t_[:], 0.0, [nm])
              ms("pool", t_[:, r0:r0 + 16], -1.0, [nm])
              asel(t_[:, r0:r0 + 16], [[-1, 16]], ALU.is_equal, -16 - r0, 1, nm)
              ms("pool", t_[:, r0 + 16:r0 + 32], 1.0, [nm])
              asel(t_[:, r0 + 16:r0 + 32], [[-1, 16]], ALU.is_equal, -r0, 1, nm)
          ms("pool", padI[:], 0.0, ["padI"])
          ms("pool", padI[:, 64:96], 1.0, ["padI"])
          asel(padI[:, 64:96], [[-1, 32]], ALU.is_equal, 0, 1, "padI")
          if stop == "const3":
              ms("pool", xt[:], 0.0, ["xt"])
          ck("const3", 0, 0, slice(0, TT), outT, "outT")
          ms("pool", vc[:], 1.0, ["vc"])
          if stop == "const4":
              ms("pool", xt[:], 0.0, ["xt"])
          ck("const4", 0, 0, slice(0, TT), outT, "outT")
          dma("sp", pp[:], pp_d[:, :, :], [], ["pp"], "dpp")
          if stop == "const":
              ms("pool", xt[:], 0.0, ["xt"])
          ck("const", 0, 0, slice(0, TT), outT, "outT")

          cast_eng = ["dve", "pool", "dve"]
          ci_ = 0
          for i, (nm, src, P, kcn, cols) in enumerate(plist):
              hk = kcn // 2
              for half in range(2):
                  b = ci_ % 2
                  n = hk * cols
                  dma("sp", stg32[b][:P, :n].rearrange("p (c n) -> p c n", n=cols), src[:, half * hk:(half + 1) * hk, :], [],
                      [f"stg32{b}"], f"pl{b}")
                  cp(cast_eng[ci_ % 3], wbuf[b][:P, :n], stg32[b][:P, :n], [f"stg32{b}"], [f"wb{b}"])
                  dma("sp", wsc[i, :P, half * n:(half + 1) * n], wbuf[b][:P, :n], [f"wb{b}"], [("wsc", nm)], f"ps{b}")
                  ci_ += 1

          if stop == "prologue":
              ms("pool", xt[:], 0.0, ["xt"])
          ck("prologue", 0, 0, slice(0, TT), outT, "outT")
          NT_ = S // TT
          tile_order = lambda l: ([f"{nm}_{l}" for nm in ("C", "A0", "A1", "B0", "B1", "D0", "D1", "D2", "O0", "O1", "O2", "O3")]
                                  + [f"{g}{j}_{l}" for j in range(6) for g in ("G", "U")] + [f"W{j}_{l}" for j in range(8)])
          gorder = [(l, n, nm) for l in range(DEPTH) for n in range(NT_) for nm in tile_order(l)]
          gindex = {inst: i for i, inst in enumerate(gorder)}
          wstate = {"nxt": 0, "owner": [None] * NWB, "map": {}, "cur": (0, 0)}

          def wload(inst, j):
              i, P, kcn, cols = pieces[inst[2]]
              wstate["owner"][j] = inst
              wstate["map"][inst] = j
              dma("sp", wbuf[j][:P, :kcn * cols], wsc[i, :P, :kcn * cols], [("wsc", inst[2])], [f"wb{j}"], f"wl{j}")

          def need(*names):
              live = [wstate["cur"] + (nm,) for nm in names]
              lo = min(gindex[i] for i in live)
              hi = max(gindex[i] for i in live)

              def free_buf():
                  for j in range(NWB):
                      o = wstate["owner"][j]
                      if o is None or gindex[o] < lo:
                          return j
                  return None
              while wstate["nxt"] < len(gorder):
                  j = free_buf()
                  if j is None:
                      assert wstate["nxt"] > hi, "weight buffers exhausted"
                      break
                  wload(gorder[wstate["nxt"]], j)
                  wstate["nxt"] += 1
                  if wstate["nxt"] > hi + NWB:
                      break

          def wview(nm):
              inst = wstate["cur"] + (nm,)
              i, P, kcn, cols = pieces[nm]
              j = wstate["map"][inst]
              assert wstate["owner"][j] == inst
              return wbuf[j][:P, :kcn * cols].rearrange("p (c n) -> p c n", n=cols), f"wb{j}"

          def lin_fm(nm, lo, hi, rhs_fn, nk, out_ps, okey, rkeys):
              wvw, wk = wview(nm)
              for k in range(nk):
                  mm(out_ps, wvw[:, k, lo:hi], rhs_fn(k), [wk] + rkeys, [okey], start=(k == 0), stop=(k == nk - 1))

          def rmsnorm_x(gcol, l):
              for c in range(8):
                  act(sqb[:], xt[:, c, :], AF.Square, ["xt"], ["sqb"])
                  mm(pss[:], ones[:], sqb[:], ["ones", "sqb"], ["pss"], start=(c == 0), stop=(c == 7))
              rs_from_ss(pss[:], 1024.0, 128)
              for c in range(8):
                  stt("dve", hb[:, c, :], xt[:, c, :], pp[:, l, gcol + c:gcol + c + 1], rstd[:],
                      ALU.mult, ALU.mult, ["xt", "pp", "rstd"], ["hb"])

          def post_norm_residual(gcol, l):
              for c in range(8):
                  act(sqb[:], yb[:, c, :], AF.Square, ["yb"], ["sqb"])
                  mm(pss[:], ones[:], sqb[:], ["ones", "sqb"], ["pss"], start=(c == 0), stop=(c == 7))
              rs_from_ss(pss[:], 1024.0, 128)
              for c in range(8):
                  stt("dve", yb[:, c, :], yb[:, c, :], pp[:, l, gcol + c:gcol + c + 1], rstd[:], ALU.mult, ALU.mult,
                      ["yb", "pp", "rstd"], ["yb"])
                  tt("pool", xt[:, c, :], xt[:, c, :], yb[:, c, :], ALU.add, ["xt", "yb"], ["xt"])

          def head_out(po_ap, gate_ap, gkeys, ngcol, l, mi):
              act(tb[7][:64, :], po_ap, AF.Square, ["pq3"], ["tb7"])
              mm(pss[:64, :], ones[:64, :64], tb[7][:64, :], ["ones", "tb7"], ["pss"])
              rs_from_ss(pss[:64, :], 64.0, 64)
              stt("dve", tf[7][:64, :], po_ap, pp[:64, l, ngcol:ngcol + 1], rstd[:64, :], ALU.mult, ALU.mult,
                  ["pq3", "pp", "rstd"], ["tf7"])
              tt("dve", mix[:, mi, :], tf[7][:64, :], gate_ap, ALU.mult, ["tf7"] + gkeys, ["mix"])
              tk_["f"]()

          hbr = lambda k: hb[:, k, :]

          tk_ = {"f": lambda: None}

          def chunk_gla(dk, q_ap, k_ap, g_ap, keys, gscale, qscale, S32, hcol):
              b_ = tf[3][:dk, :]
              add("dve", lambda e: e.tensor_tensor_scan(out=b_, data0=seg[:dk, :], data1=g_ap, initial=0.0,
                                                        op0=ALU.mult, op1=ALU.add), ["seg"] + keys, ["tf3"])
              b3 = b_.rearrange("p (c t) -> p c t", t=64)
              act(tf[4][:dk, :], b_, AF.Exp, ["tf3"], ["tf4"], scale=gscale)
              stt("dve", tb[0][:dk, :], q_ap, qscale, tf[4][:dk, :], ALU.mult, ALU.mult, keys + ["tf4"], ["tb0"])
              ts("dve", tf[4][:dk, :], b_, -gscale, ALU.mult, ["tf3"], ["tf4"], s2=80.0, op1=ALU.min)
              act(tf[4][:dk, :], tf[4][:dk, :], AF.Exp, ["tf4"], ["tf4"])
              tt("dve", tb[1][:dk, :], k_ap, tf[4][:dk, :], ALU.mult, keys + ["tf4"], ["tb1"])
              tk_["f"]()
              cp("dve", blast[:dk, :], b3[:, :, 63], ["tf3"], ["blast"])
              tt("dve", tf[4][:dk, :].rearrange("p (c t) -> p c t", t=64), blast[:dk, :].unsqueeze(2).to_broadcast([dk, NCH, 64]),
                 b3, ALU.subtract, ["blast", "tf3"], ["tf4"])
              act(tf[4][:dk, :], tf[4][:dk, :], AF.Exp, ["tf4"], ["tf4"], scale=gscale)
              tt("dve", tb[2][:dk, :], k_ap, tf[4][:dk, :], ALU.mult, keys + ["tf4"], ["tb2"])
              act(acol[:dk, :], blast[:dk, :], AF.Exp, ["blast"], ["acol"], scale=gscale)
              tk_["f"]()
              for c in range(NCH):
                  tr(ptb[0][:64, c * dk:(c + 1) * dk], tb[2][:dk, c * 64:(c + 1) * 64], ident[:dk, :dk], ["tb2", "ident"], ["ptb0"])
              cp("act", k2tok[:, :, :dk], ptb[0][:64, :NCH * dk].rearrange("p (c d) -> p c d", d=dk), ["ptb0"], ["k2tok"])
              tk_["f"]()
              for c in range(NCH):
                  mm(pq[0][:64, c * 64:(c + 1) * 64], tb[1][:dk, c * 64:(c + 1) * 64], tb[0][:dk, c * 64:(c + 1) * 64],
                     ["tb0", "tb1"], ["pq0"])
              tt("dve", attT[:], pq[0][:64, :].rearrange("p (c t) -> p c t", t=64), mU[:], ALU.mult, ["pq0", "mU"], ["attT"])
              tk_["f"]()
              for c in range(NCH):
                  mm(pq[1][:dk, c * 64:(c + 1) * 64], k2tok[:, c, :dk], vtok[:, c, hcol:hcol + 64], ["k2tok", "vtok"], ["pq1"])
              for c in range(NCH):
                  cp("dve", Sbf[:dk, c, :], S32[:], [S32.name], ["Sbf"])
                  stt("dve", S32[:], S32[:], acol[:dk, c:c + 1], pq[1][:dk, c * 64:(c + 1) * 64], ALU.mult, ALU.add,
                      [S32.name, "acol", "pq1"], [S32.name])
                  tk_["f"]()
              for c in range(NCH):
                  o_ = pq[3][:64, c * 64:(c + 1) * 64]
                  mm(o_, vtok[:, c, hcol:hcol + 64], attT[:, c, :], ["vtok", "attT"], ["pq3"], start=True, stop=False)
                  mm(o_, Sbf[:dk, c, :], tb[0][:dk, c * 64:(c + 1) * 64], ["Sbf", "tb0"], ["pq3"], start=False, stop=True)

          def rope(ps_ap, R, tab, tkey, rot, rkey, out_ap, okey, pkey):
              cp("act", tf[6][:R, :], ps_ap, [pkey], ["tf6"])
              cp("pool", tb[6][:R, :], tf[6][:R, :], ["tf6"], ["tb6"])
              mm(pq[0][:R, :], rot[:R, :R], tb[6][:R, :], [rkey, "tb6"], ["pq0"])
              tt("dve", tf[5][:R, :], pq[0][:R, :], tab[:, 1, :], ALU.mult, ["pq0", tkey], ["tf5"])
              tt("dve", tf[6][:R, :], tf[6][:R, :], tab[:, 0, :], ALU.mult, ["tf6", tkey], ["tf6"])
              tt("pool", out_ap, tf[6][:R, :], tf[5][:R, :], ALU.add, ["tf5", "tf6"], [okey])

          for l in range(DEPTH):
              src = xT if l == 0 else xs[(l - 1) % 2]
              dst = outT if l == DEPTH - 1 else xs[l % 2]
              skey = "xT" if l == 0 else f"xs{(l - 1) % 2}"
              dkey = "outT" if l == DEPTH - 1 else f"xs{l % 2}"
              for t_ in Sa32 + Sb32 + Sd32:
                  ms("pool", t_[:], 0.0, [t_.name])
              ms("pool", ctail[:], 0.0, ["ctail"])
              ms("pool", gtail[:], 0.0, ["gtail"])
              dma("sp", stg32[0][:, :768].rearrange("p (c n) -> p c n", n=384), w_uq[l].rearrange("(c p) n -> p c n", p=128),
                  [], ["stg320"], "dsm")
              cp("dve", wuq[:], stg32[0][:, :768].rearrange("p (c n) -> p c n", n=384), ["stg320"], ["wuq"])
              dma("sp", stg32[1][:, :512], w_ukv[l], [], ["stg321"], "dsm2")
              ms("pool", wkpad[:], 0.0, ["wkpad"])
              s4 = stg32[1][:, :512].rearrange("p (h n) -> p h n", n=128)
              cp("dve", wkpad[:, :, 0:64], s4[:, :, 0:64], ["stg321"], ["wkpad"])
              cp("dve", wv[:], s4[:, :, 64:128], ["stg321"], ["wv"])
              dma("sp", stg32[0][:16, 1024:1152], w_gk2[l], [], ["stg320"], "dsm")
              cp("dve", wgk[:], stg32[0][:16, 1024:1152], ["stg320"], ["wgk"])
              if l == 0:
                  ms("pool", lbt[:], 0.0, ["lbt"])
              else:
                  tt("dve", lbt[:], pp[:64, l, 36:40], pp[:64, l, 32:36], ALU.subtract, ["pp"], ["lbt"])
                  act(lbt[:], lbt[:], AF.Sigmoid, ["lbt"], ["lbt"])
              ts("dve", oml[:], lbt[:], -1.0, ALU.mult, ["lbt"], ["oml"], s2=1.0, op1=ALU.add)
              ts("dve", noml[:], oml[:], -1.0, ALU.mult, ["oml"], ["noml"])
              act(negA[:, l:l + 1], pp[:4, l, 98:99], AF.Exp, ["pp"], ["negA"])
              ts("dve", negA[:, l:l + 1], negA[:, l:l + 1], -1.0, ALU.mult, ["negA"], ["negA"])

              for n in range(NT):
                  t0 = n * TT
                  tsl = slice(t0, t0 + TT)
                  wstate["cur"] = (l, n)
                  if stop == "setup":
                      ms("pool", xt[:], 0.0, ["xt"])
                  ck("setup", l, n, tsl, dst, dkey)
                  dma("sp", xt[:], src.rearrange("(c p) t -> p c t", p=128)[:, :, tsl], [(skey, n)], ["xt"], "dx")
                  dma("sp", tq[:], tabq[:, :, tsl], [], ["tq"], "dtq")
                  dma("sp", tk[:], tabk[:, :, tsl], [], ["tk"], "dtk")
                  ck("load", l, n, tsl, dst, dkey)
                  rmsnorm_x(0, l)

                  need(f"C_{l}")
                  for c in range(2):
                      lin_fm(f"C_{l}", c * 128, c * 128 + 128, hbr, 8, pd[c][:, :], f"pd{c}", ["hb"])
                      cp("act", cqf[:, c, :], pd[c][:, :], [f"pd{c}"], ["cqf"])
                      act(sqb[:], pd[c][:, :], AF.Square, [f"pd{c}"], ["sqb"])
                      mm(pss[:], ones[:], sqb[:], ["ones", "sqb"], ["pss"], start=(c == 0), stop=(c == 1))
                  rs_from_ss(pss[:], 256.0, 128)
                  for c in range(2):
                      stt("dve", cqn[:, c, :], cqf[:, c, :], pp[:, l, 47 + c:48 + c], rstd[:], ALU.mult, ALU.mult,
                          ["cqf", "pp", "rstd"], ["cqn"])
                  lin_fm(f"C_{l}", 256, 384, hbr, 8, pd[0][:, :], "pd0", ["hb"])
                  act(sqb[:], pd[0][:, :], AF.Square, ["pd0"], ["sqb"])
                  mm(pss[:], ones[:], sqb[:], ["ones", "sqb"], ["pss"])
                  rs_from_ss(pss[:], 128.0, 128)
                  stt("dve", ckvn[:], pd[0][:, :], pp[:, l, 49:50], rstd[:], ALU.mult, ALU.mult, ["pd0", "pp", "rstd"], ["ckvn"])
                  lin_fm(f"C_{l}", 384, 416, hbr, 8, pd[1][:32, :], "pd1", ["hb"])
                  rope(pd[1][:32, :], 32, tk, "tk", rotk, "rotk", krope[:], "krope", "pd1")
                  ck("C1", l, n, tsl, dst, dkey)
                  for h in range(4):
                      mm(pd[h % 2][:96, :], wkpad[:, h, :], ckvn[:], ["wkpad", "ckvn"], [f"pd{h % 2}"], start=True, stop=False)
                      mm(pd[h % 2][:96, :], padI[:], krope[:], ["padI", "krope"], [f"pd{h % 2}"], start=False, stop=True)
                      cp("act", ksb[h % 2][:], pd[h % 2][:96, :], [f"pd{h % 2}"], [f"ksb{h % 2}"])
                      dma("pool", kd[h][:, tsl], ksb[h % 2][:], [f"ksb{h % 2}"], [f"kd{h}"], f"kst{h % 2}")
                  ck("C2", l, n, tsl, dst, dkey)
                  for m in range(NBK):
                      mm(pd[m % 2][:, :], ckvn[:, m * 128:(m + 1) * 128], wv[:].rearrange("p h n -> p (h n)"), ["ckvn", "wv"],
                         [f"pd{m % 2}"])
                      cp("act", vc[:, n * NBK + m, :].rearrange("p (h n) -> p h n", n=65)[:, :, 0:64],
                         pd[m % 2][:, :].rearrange("p (h n) -> p h n", n=64), [f"pd{m % 2}"], ["vc"])
                  ck("C3", l, n, tsl, dst, dkey)
                  for h in range(4):
                      for c in range(2):
                          mm(pd[h % 2][:96, :], wuq[:, c, h * 96:(h + 1) * 96], cqn[:, c, :], ["wuq", "cqn"], [f"pd{h % 2}"],
                             start=(c == 0), stop=(c == 1))
                      ck(f"Q{h}a", l, n, tsl, dst, dkey)
                      rope(pd[h % 2][:96, :], 96, tq, "tq", rotq, "rotq", qTh[h][:], f"qTh{h}", f"pd{h % 2}")
                      ck(f"Q{h}b", l, n, tsl, dst, dkey)
                  ck("C4", l, n, tsl, dst, dkey)
                  scale = 96.0 ** -0.5
                  nkb = (n + 1) * NBK

                  def pv(o_, l_, r_, s_, e_, pk):
                      add("pe", lambda e: e.matmul(o_, lhsT=l_, rhs=r_, start=s_, stop=e_, skip_group_check=True), ["vc", pk],
                          ["po", "pss2"] if s_ else ["po"])

                  def att_gen():
                      ai = 0
                      ki = 0
                      for h in range(4):
                          for kb in range(nkb):
                              if kb % (KP // 128) == 0:
                                  kbf_ = kbuf[ki % 2]
                                  kk = f"kbuf{ki % 2}"
                                  ki += 1
                                  nk_ = min(KP, (nkb - kb) * 128)
                                  dma("sp", kbf_[:, :nk_], kd[h][:, kb * 128:kb * 128 + nk_], [f"kd{h}"], [kk], "kl" + kk[-1])
                              ko = (kb % (KP // 128)) * 128
                              m = kb - n * NBK
                              qlo = max(m, 0) * 128
                              sa = psa[ai % 2]
                              sk = f"psa{ai % 2}"
                              pt_ = pT[ai % 2]
                              pk = f"pT{ai % 2}"
                              ai += 1
                              mm(sa[:, qlo:TT], kbf_[:, ko:ko + 128], qTh[h][:, qlo:TT], [kk, f"qTh{h}"], [sk])
                              act(pt_[:, qlo:TT], sa[:, qlo:TT], AF.Exp, [sk], [pk], scale=scale)
                              if m >= 0:
                                  tt("pool", pt_[:, qlo:qlo + 128], pt_[:, qlo:qlo + 128], tri[:], ALU.mult, [pk, "tri"], [pk])
                              pv(po[:65, qlo:TT], vc[:, kb, h * 65:(h + 1) * 65], pt_[:, qlo:TT], kb == 0, kb == nkb - 1, pk)
                              yield
                          cp("act", ob[:], po[:65, :], ["po"], ["ob"])
                          mm(pss2[:64, :], Esel[:], ob[:], ["Esel", "ob"], ["pss2"])
                          add("dve", lambda e: e.reciprocal(out=tfa[:], in_=pss2[:64, :]), ["pss2"], ["tfa"])
                          tt("dve", mix[:, 8 + h, :], ob[:64, :], tfa[:], ALU.mult, ["ob", "tfa"], ["mix"])
                          yield
                  att = att_gen()
                  per_tick = max(1, -(-(4 * nkb + 4) // 110))

                  def tick():
                      for _ in range(per_tick):
                          if next(att, "done") == "done":
                              break
                  tk_["f"] = tick
                  ck("norm", l, n, tsl, dst, dkey)
                  need(f"A0_{l}", f"A1_{l}")
                  wvw, wk = wview(f"A1_{l}")
                  for c in range(NCH):
                      for k in range(8):
                          mm(pd[c % 2][:64, :], hb[:, k, c * 64:(c + 1) * 64], wvw[:, k, 0:256], ["hb", wk], [f"pd{c % 2}"],
                             start=(k == 0), stop=(k == 7))
                      cp("act", vtok[:, c, :], pd[c % 2][:64, :], [f"pd{c % 2}"], ["vtok"])
                  for h in range(4):
                      lin_fm(f"A0_{l}", h * 64, h * 64 + 64, hbr, 8, pd[0][:64, :], "pd0", ["hb"])
                      lin_fm(f"A0_{l}", 256 + h * 64, 256 + h * 64 + 64, hbr, 8, pd[1][:64, :], "pd1", ["hb"])
                      act(tf[0][:64, :], pd[0][:64, :], AF.Silu, ["pd0"], ["tf0"])
                      act(tf[1][:64, :], pd[1][:64, :], AF.Sigmoid, ["pd1"], ["tf1"])
                      ts("dve", tf[2][:64, :], tf[1][:64, :], noml[:, h:h + 1], ALU.mult, ["tf1", "noml", "oml"], ["tf2"],
                         s2=oml[:, h:h + 1], op1=ALU.add)
                      ts("dve", tf[1][:64, :], tf[1][:64, :], oml[:, h:h + 1], ALU.mult, ["tf1", "oml", "lbt"], ["tf1"],
                         s2=lbt[:, h:h + 1], op1=ALU.add)
                      act(tf[1][:64, :], tf[1][:64, :], AF.Ln, ["tf1"], ["tf1"])
                      tk_["f"]()
                      chunk_gla(64, tf[0][:64, :], tf[2][:64, :], tf[1][:64, :], ["tf0", "tf1", "tf2"], 1.0, 0.125, Sa32[h], h * 64)
                      lin_fm(f"A1_{l}", 256 + h * 64, 256 + h * 64 + 64, hbr, 8, pd[0][:64, :], "pd0", ["hb"])
                      act(tf[0][:64, :], pd[0][:64, :], AF.Sigmoid, ["pd0"], ["tf0"])
                      head_out(pq[3][:64, :], tf[0][:64, :], ["tf0"], 40, l, h)

                  ck("A", l, n, tsl, dst, dkey)
                  need(f"B0_{l}", f"B1_{l}")
                  wvw, wk = wview(f"B0_{l}")
                  for c in range(NCH):
                      for k in range(8):
                          mm(pd[c % 2][:64, :], hb[:, k, c * 64:(c + 1) * 64], wvw[:, k, 256:512], ["hb", wk], [f"pd{c % 2}"],
                             start=(k == 0), stop=(k == 7))
                      cp("act", vtok[:, c, :], pd[c % 2][:64, :], [f"pd{c % 2}"], ["vtok"])
                  lin_fm(f"B1_{l}", 0, 16, hbr, 8, pd[0][:16, :], "pd0", ["hb"])
                  cp("act", codeb[:], pd[0][:16, :], ["pd0"], ["codeb"])
                  for h in range(4):
                      lin_fm(f"B0_{l}", h * 32, h * 32 + 32, hbr, 8, pd[0][:32, :], "pd0", ["hb"])
                      lin_fm(f"B0_{l}", 128 + h * 32, 128 + h * 32 + 32, hbr, 8, pd[1][:32, :], "pd1", ["hb"])
                      cp("act", tf[0][:32, :], pd[0][:32, :], ["pd0"], ["tf0"])
                      cp("act", tf[2][:32, :], pd[1][:32, :], ["pd1"], ["tf2"])
                      mm(pd[0][:32, :], wgk[:, h * 32:h * 32 + 32], codeb[:], ["wgk", "codeb"], ["pd0"])
                      act(tf[1][:32, :], pd[0][:32, :], AF.Sigmoid, ["pd0", "pp"], ["tf1"], bias=pp[:32, l, 43 + h:44 + h])
                      act(tf[1][:32, :], tf[1][:32, :], AF.Ln, ["tf1"], ["tf1"])
                      tk_["f"]()
                      chunk_gla(32, tf[0][:32, :], tf[2][:32, :], tf[1][:32, :], ["tf0", "tf1", "tf2"], 1.0 / 16.0, 32.0 ** -0.5,
                                Sb32[h], h * 64)
                      lin_fm(f"B1_{l}", 16 + h * 64, 16 + h * 64 + 64, hbr, 8, pd[0][:64, :], "pd0", ["hb"])
                      act(tf[0][:64, :], pd[0][:64, :], AF.Silu, ["pd0"], ["tf0"])
                      head_out(pq[3][:64, :], tf[0][:64, :], ["tf0"], 41, l, 4 + h)

                  ck("B", l, n, tsl, dst, dkey)
                  ck("C", l, n, tsl, dst, dkey)
                  need(f"D0_{l}", f"D1_{l}", f"D2_{l}")
                  lin_fm(f"D2_{l}", 0, 4, hbr, 8, pd[0][:4, :], "pd0", ["hb"])
                  act(bet4[:], pd[0][:4, :], AF.Sigmoid, ["pd0"], ["bet4"])
                  lin_fm(f"D2_{l}", 4, 8, hbr, 8, pd[1][:4, :], "pd1", ["hb"])
                  act(lg4[:], pd[1][:4, :], AF.Exp, ["pd1", "pp"], ["lg4"], bias=pp[:4, l, 99:100])
                  act(lg4[:], lg4[:], AF.Ln, ["lg4"], ["lg4"], bias=1.0)
                  ts("dve", lg4[:], lg4[:], negA[:, l:l + 1], ALU.mult, ["lg4", "negA"], ["lg4"])
                  add("dve", lambda e: e.tensor_tensor_scan(out=b4[:], data0=seg[:4, :], data1=lg4[:], initial=0.0,
                                                            op0=ALU.mult, op1=ALU.add), ["seg", "lg4"], ["b4"])
                  for c in range(NCH):
                      mm(pq[0][:64, c * 4:(c + 1) * 4], b4[:, c * 64:(c + 1) * 64], identf[:4, :4], ["b4", "identf"], ["pq0"])
                  cp("dve", btok[:], pq[0][:64, :NCH * 4].rearrange("p (c h) -> p c h", h=4), ["pq0"], ["btok"])
                  for h in range(4):
                      mm(pq[2][:64, :], sel[:, h, :], b4[:], ["sel", "b4"], ["pq2"])
                      cp("act", tf[3][:64, :], pq[2][:64, :], ["pq2"], ["tf3"])
                      mm(pq[2][:64, :], sel[:, h, :], bet4[:], ["sel", "bet4"], ["pq2"])
                      cp("act", tf[4][:64, :], pq[2][:64, :], ["pq2"], ["tf4"])
                      bB3 = tf[3][:64, :].rearrange("p (c t) -> p c t", t=64)
                      bt_bc = btok[:, :, h:h + 1].to_broadcast([64, NCH, 64])
                      tt("dve", lmT[:], bB3, bt_bc, ALU.subtract, ["tf3", "btok"], ["lmT"])
                      ts("dve", lmT[:], lmT[:], 0.0, ALU.min, ["lmT"], ["lmT"])
                      act(lmT[:], lmT[:], AF.Exp, ["lmT"], ["lmT"])
                      tt("dve", lm[:], bt_bc, bB3, ALU.subtract, ["tf3", "btok"], ["lm"])
                      ts("dve", lm[:], lm[:], 0.0, ALU.min, ["lm"], ["lm"])
                      act(lm[:], lm[:], AF.Exp, ["lm"], ["lm"])
                      tk_["f"]()
                      for qi, (pn, lo) in enumerate(((f"D0_{l}", h * 64), (f"D0_{l}", 256 + h * 64), (f"D1_{l}", h * 64))):
                          ci = qi * 4 + h
                          lin_fm(pn, lo, lo + 64, hbr, 8, pd[qi % 2][:64, :], f"pd{qi % 2}", ["hb"])
                          cp("pool", cbuf[:, 0:3], ctail[:, ci, :], ["ctail"], ["cbuf"])
                          cp("act", cbuf[:, 3:TT + 3], pd[qi % 2][:64, :], [f"pd{qi % 2}"], ["cbuf"])
                          cw = lambda j: pp[:64, l, 50 + ci * 4 + j:51 + ci * 4 + j]
                          ts("dve", dq[qi][:], cbuf[:, 0:TT], cw(0), ALU.mult, ["cbuf", "pp"], [f"dq{qi}"])
                          for j in (1, 2, 3):
                              stt("dve", dq[qi][:], cbuf[:, j:TT + j], cw(j), dq[qi][:], ALU.mult, ALU.add, ["cbuf", "pp", f"dq{qi}"],
                                  [f"dq{qi}"])
                          cp("pool", ctail[:, ci, :], cbuf[:, TT:TT + 3], ["cbuf"], ["ctail"])
                          tk_["f"]()
                          act(dq[qi][:], dq[qi][:], AF.Silu, [f"dq{qi}"], [f"dq{qi}"])
                          if qi < 2:
                              act(tb[7][:64, :], dq[qi][:], AF.Square, [f"dq{qi}"], ["tb7"])
                              mm(pss[:64, :], ones[:64, :64], tb[7][:64, :], ["ones", "tb7"], ["pss"])
                              act(rstd[:64, :], pss[:64, :], AF.Sqrt, ["pss"], ["rstd"], bias=EPS, scale=1.0)
                              add("dve", lambda e: e.reciprocal(out=rstd[:64, :], in_=rstd[:64, :]), ["rstd"], ["rstd"])
                              stt("dve", dq[qi][:], dq[qi][:], 0.125 if qi == 0 else 1.0, rstd[:64, :], ALU.mult, ALU.mult,
                                  [f"dq{qi}", "rstd"], [f"dq{qi}"])
                      cp("pool", qbf[:], dq[0][:], ["dq0"], ["qbf"])
                      cp("pool", kbf[:], dq[1][:], ["dq1"], ["kbf"])
                      tt("dve", tf[5][:64, :], dq[1][:], tf[4][:64, :], ALU.mult, ["dq1", "tf4"], ["tf5"])
                      cp("pool", kbb[:], tf[5][:64, :], ["tf5"], ["kbb"])
                      act(tf[6][:64, :], tf[3][:64, :], AF.Exp, ["tf3"], ["tf6"])
                      tt("dve", q2b[:], dq[0][:], tf[6][:64, :], ALU.mult, ["dq0", "tf6"], ["q2b"])
                      tt("dve", tb[3][:64, :], tf[5][:64, :], tf[6][:64, :], ALU.mult, ["tf5", "tf6"], ["tb3"])
                      tt("dve", tb[4][:64, :], dq[2][:], tf[4][:64, :], ALU.mult, ["dq2", "tf4"], ["tb4"])
                      cp("dve", blast[:, :], bB3[:, :, 63], ["tf3"], ["blast"])
                      act(elast[:], blast[:], AF.Exp, ["blast"], ["elast"])
                      tt("dve", tf[6][:64, :].rearrange("p (c t) -> p c t", t=64), blast[:].unsqueeze(2).to_broadcast([64, NCH, 64]),
                         bB3, ALU.subtract, ["blast", "tf3"], ["tf6"])
                      act(tf[6][:64, :], tf[6][:64, :], AF.Exp, ["tf6"], ["tf6"])
                      tt("dve", tb[5][:64, :], dq[1][:], tf[6][:64, :], ALU.mult, ["dq1", "tf6"], ["tb5"])
                      tk_["f"]()
                      for src_t, skey_t, dst_t, dkey_t, pi in ((tb[3], "tb3", kbetok, "kbetok", 0), (tb[4], "tb4", vbtok, "vbtok", 1),
                                                               (tb[5], "tb5", k2tok, "k2tok", 0)):
                          for c in range(NCH):
                              tr(ptb[pi][:64, c * 64:(c + 1) * 64], src_t[:64, c * 64:(c + 1) * 64], ident[:64, :64],
                                 [skey_t, "ident"], [f"ptb{pi}"])
                          cp("act", dst_t[:], ptb[pi][:64, :NCH * 64].rearrange("p (c d) -> p c d", d=64), [f"ptb{pi}"], [dkey_t])
                          tk_["f"]()
                      for c in range(NCH):
                          cs = slice(c * 64, (c + 1) * 64)
                          mm(pq[0][:64, cs], kbb[:, cs], kbf[:, cs], ["kbb", "kbf"], ["pq0"])
                          mm(pq[1][:64, cs], kbf[:, cs], kbb[:, cs], ["kbb", "kbf"], ["pq1"])
                          mm(pq[2][:64, cs], kbf[:, cs], qbf[:, cs], ["kbf", "qbf"], ["pq2"])
                      v3 = lambda p_: p_[:64, :].rearrange("p (c t) -> p c t", t=64)
                      tt("dve", tf[5][:64, :].rearrange("p (c t) -> p c t", t=64), lm[:], mLs[:], ALU.mult, ["lm", "mLs"], ["tf5"])
                      tt("dve", Zb[0][:], v3(pq[0]), tf[5][:64, :].rearrange("p (c t) -> p c t", t=64), ALU.mult, ["pq0", "tf5"], ["Zb0"])
                      tt("dve", tf[5][:64, :].rearrange("p (c t) -> p c t", t=64), lmT[:], mUs[:], ALU.mult, ["lmT", "mUs"], ["tf5"])
                      tt("dve", Yb[0][:], v3(pq[1]), tf[5][:64, :].rearrange("p (c t) -> p c t", t=64), ALU.mult, ["pq1", "tf5"], ["Yb0"])
                      tt("dve", tf[5][:64, :].rearrange("p (c t) -> p c t", t=64), lmT[:], mU[:], ALU.mult, ["lmT", "mU"], ["tf5"])
                      tt("dve", aqk[:], v3(pq[2]), tf[5][:64, :].rearrange("p (c t) -> p c t", t=64), ALU.mult, ["pq2", "tf5"], ["aqk"])
                      tt("dve", P32[:], Yb[0][:], identf[:64, :64].unsqueeze(1).to_broadcast([64, NCH, 64]), ALU.add,
                         ["Yb0", "identf"], ["P32"])
                      cp("pool", Pb[:], P32[:], ["P32"], ["Pb"])
                      tk_["f"]()
                      for lv in range(5):
                          a_, b_i = lv % 2, (lv + 1) % 2
                          for c in range(NCH):
                              cs = slice(c * 64, (c + 1) * 64)
                              mm(pq[0][:64, cs], Yb[a_][:, c, :], Zb[a_][:, c, :], [f"Yb{a_}", f"Zb{a_}"], ["pq0"])
                          cp("act", Zb[b_i][:], v3(pq[0]), ["pq0"], [f"Zb{b_i}"])
                          if lv < 4:
                              for c in range(NCH):
                                  cs = slice(c * 64, (c + 1) * 64)
                                  mm(pq[1][:64, cs], Zb[a_][:, c, :], Yb[a_][:, c, :], [f"Yb{a_}", f"Zb{a_}"], ["pq1"])
                              cp("act", Yb[b_i][:], v3(pq[1]), ["pq1"], [f"Yb{b_i}"])
                          for c in range(NCH):
                              cs = slice(c * 64, (c + 1) * 64)
                              mm(pq[2][:64, cs], Zb[b_i][:, c, :], Pb[:, c, :], [f"Zb{b_i}", "Pb"], ["pq2"])
                          tt("dve", P32[:], P32[:], v3(pq[2]), ALU.add, ["P32", "pq2"], ["P32"])
                          cp("pool", Pb[:], P32[:], ["P32"], ["Pb"])
                          tk_["f"]()
                      for c in range(NCH):
                          cs = slice(c * 64, (c + 1) * 64)
                          mm(pq[0][:64, cs], Pb[:, c, :], vbtok[:, c, :], ["Pb", "vbtok"], ["pq0"])
                          mm(pq[1][:64, cs], kbetok[:, c, :], Pb[:, c, :], ["Pb", "kbetok"], ["pq1"])
                      cp("act", ud[:], v3(pq[0]), ["pq0"], ["ud"])
                      cp("act", wTd[:], v3(pq[1]), ["pq1"], ["wTd"])
                      tk_["f"]()
                      S32 = Sd32[h]
                      Sb_ = Sdbf[h]
                      for c in range(NCH):
                          cs = slice(c * 64, (c + 1) * 64)
                          cp("dve", Sb_[:], S32[:], [S32.name], [Sb_.name])
                          mm(pq[0][:64, 0:64], wTd[:, c, :], Sb_[:], ["wTd", Sb_.name], ["pq0"])
                          tt("dve", vnew[:], ud[:, c, :], pq[0][:64, 0:64], ALU.subtract, ["ud", "pq0"], ["vnew"])
                          mm(pq[3][:64, cs], Sb_[:], q2b[:, cs], [Sb_.name, "q2b"], ["pq3"], start=True, stop=False)
                          mm(pq[3][:64, cs], vnew[:], aqk[:, c, :], ["vnew", "aqk"], ["pq3"], start=False, stop=True)
                          mm(pq[2][:64, 0:64], k2tok[:, c, :], vnew[:], ["k2tok", "vnew"], ["pq2"])
                          stt("dve", S32[:], S32[:], elast[:, c:c + 1], pq[2][:64, 0:64], ALU.mult, ALU.add,
                              [S32.name, "elast", "pq2"], [S32.name])
                          tk_["f"]()
                      lin_fm(f"D2_{l}", 8 + h * 64, 8 + h * 64 + 64, hbr, 8, pd[0][:64, :], "pd0", ["hb"])
                      act(tf[0][:64, :], pd[0][:64, :], AF.Silu, ["pd0"], ["tf0"])
                      head_out(pq[3][:64, :], tf[0][:64, :], ["tf0"], 42, l, 12 + h)

                  for _ in att:
                      pass
                  tk_["f"] = lambda: None
                  ck("D", l, n, tsl, dst, dkey)
                  for j in range(4):
                      need(f"O{j}_{l}")
                      wvw, wk = wview(f"O{j}_{l}")
                      for cc in range(2):
                          oc = j * 2 + cc
                          for k in range(16):
                              mm(pd[oc % 2][:, :], wvw[:, k, cc * 128:(cc + 1) * 128], mix[:, k, :], [wk, "mix"], [f"pd{oc % 2}"],
                                 start=(k == 0), stop=(k == 15))
                          cp("act", yb[:, oc, :], pd[oc % 2][:, :], [f"pd{oc % 2}"], ["yb"])
                  post_norm_residual(8, l)

                  ck("O", l, n, tsl, dst, dkey)
                  rmsnorm_x(16, l)
                  for c in range(NFF):
                      j, cc = c // 4, c % 4
                      need(f"G{j}_{l}", f"U{j}_{l}")
                      pg, kg, pu, ku = ((pd[0], "pd0", pd[1], "pd1"), (psa[0], "psa0", psa[1], "psa1"))[c % 2]
                      gr, grk = ((graw, "graw"), (graw2, "graw2"))[c % 2]
                      ta, tak, tb_, tbk = ((tf[0], "tf0", tf[1], "tf1"), (tf[2], "tf2", tf[3], "tf3"))[c % 2]
                      lin_fm(f"G{j}_{l}", cc * 128, cc * 128 + 128, hbr, 8, pg[:, :], kg, ["hb"])
                      lin_fm(f"U{j}_{l}", cc * 128, cc * 128 + 128, hbr, 8, pu[:, :], ku, ["hb"])
                      cp("pool", gr[:, 0:2], gtail[:, c, :], ["gtail"], [grk])
                      cp("act", gr[:, 2:TT + 2], pg[:, :], [kg], [grk])
                      cw = lambda jj: pp[:, l, 100 + c * 3 + jj:101 + c * 3 + jj]
                      ts("dve", ta[:], gr[:, 0:TT], cw(0), ALU.mult, [grk, "pp"], [tak])
                      stt("dve", ta[:], gr[:, 1:TT + 1], cw(1), ta[:], ALU.mult, ALU.add, [grk, "pp", tak], [tak])
                      stt("dve", ta[:], gr[:, 2:TT + 2], cw(2), ta[:], ALU.mult, ALU.add, [grk, "pp", tak], [tak])
                      cp("pool", gtail[:, c, :], gr[:, TT:TT + 2], [grk], ["gtail"])
                      act(tb_[:], ta[:], AF.Gelu_apprx_tanh, [tak], [tbk])
                      tt("dve", hid[:, c, :], tb_[:], pu[:, :], ALU.mult, [tbk, ku], ["hid"])
                  for j in range(8):
                      need(f"W{j}_{l}")
                      wvw, wk = wview(f"W{j}_{l}")
                      for k in range(NFF):
                          mm(pd[j % 2][:, :], wvw[:, k, :], hid[:, k, :], [wk, "hid"], [f"pd{j % 2}"], start=(k == 0), stop=(k == NFF - 1))
                      cp("act", yb[:, j, :], pd[j % 2][:, :], [f"pd{j % 2}"], ["yb"])
                  post_norm_residual(24, l)
                  dma("pool", dst.rearrange("(c p) t -> p c t", p=128)[:, :, tsl], xt[:], ["xt"], [(dkey, n)], "dout")

        except _Stop:
            pass
        final = [k for k in Sd.last_w if isinstance(k, tuple) and k[0] in ("outT", "xs0", "xs1")]
        Sd.emit(final_reads=final)
    return nc


def _pack_params(inp, DEPTH):
    pp = np.zeros((128, DEPTH, NPP), np.float32)
    for l in range(DEPTH):
        for i, nm in enumerate(("pre_mix_g", "post_mix_g", "pre_ffn_g", "post_ffn_g")):
            pp[:, l, i * 8:(i + 1) * 8] = np.asarray(inp[nm][l]).reshape(8, 128).T
        pp[:64, l, 32:36] = np.asarray(inp["hgrn_lb_logits"][0]).reshape(4, 64).T
        if DEPTH > 1:
            pp[:64, l, 36:40] = np.asarray(inp["hgrn_lb_logits"][1]).reshape(4, 64).T
        pp[:64, l, 40] = inp["hgrn_norm_g"][l]
        pp[:64, l, 41] = inp["gla_norm_g"][l]
        pp[:64, l, 42] = inp["gdn_norm_g"][l]
        pp[:32, l, 43:47] = np.asarray(inp["gla_b_gk"][l]).reshape(4, 32).T
        pp[:, l, 47:49] = np.asarray(inp["mla_q_norm_g"][l]).reshape(2, 128).T
        pp[:, l, 49] = inp["mla_kv_norm_g"][l]
        cw = np.asarray(inp["gdn_conv_w"][l])
        pp[:64, l, 50:98] = cw.reshape(4, 12, 64).transpose(2, 1, 0).reshape(64, 48)
        pp[:4, l, 98] = inp["gdn_a_log"][l]
        pp[:4, l, 99] = inp["gdn_dt_bias"][l]
        fw = np.asarray(inp["ffn_conv_w"][l])
        pp[:, l, 100:166] = fw.reshape(3, NFF, 128).transpose(2, 1, 0).reshape(128, 66)
    return pp


def _rope_tables(S):
    inv = (np.float32(10000.0) ** (-np.arange(0, 32, 2, dtype=np.float32) / np.float32(32))).astype(np.float32)
    ang = (np.arange(S, dtype=np.float32)[:, None] * inv[None, :]).astype(np.float32)
    cos, sin = np.cos(ang).astype(np.float32).T, np.sin(ang).astype(np.float32).T
    tabk = np.zeros((32, 2, S), np.float32)
    tabk[0:16, 0], tabk[16:32, 0] = cos, cos
    tabk[0:16, 1], tabk[16:32, 1] = sin, sin
    tabq = np.zeros((96, 2, S), np.float32)
    tabq[0:64, 0] = 1.0
    tabq[64:96] = tabk
    return tabq, tabk


_CACHE = {}


def run(inp, n_cores=None):
    x = np.asarray(inp["x"], np.float32)
    B, S, _ = x.shape
    DEPTH = int(np.asarray(inp["w_in"]).shape[0])
    key = (S, DEPTH)
    if key not in _CACHE:
        _CACHE[key] = build(S, DEPTH)
    nc = _CACHE[key]
    pp = _pack_params(inp, DEPTH)
    tabq, tabk = _rope_tables(S)
    shared = {k: np.ascontiguousarray(np.asarray(inp[k], np.float32)) for k in
              ("w_in", "w_out", "ffn_w_gate", "ffn_w_up", "ffn_w_down", "mla_w_uq", "mla_w_ukv", "gla_w_gk2")}
    shared.update(pp=pp, tabq=tabq, tabk=tabk)
    in_maps = []
    for b in range(B):
        m = dict(shared)
        m["xT"] = np.ascontiguousarray(x[b].T)
        in_maps.append(m)
    res = run_bass_kernel_spmd(nc, in_maps, core_ids=list(range(B)))
    out = np.stack([np.ascontiguousarray(r["outT"].T) for r in res.results], axis=0)
    return out.astype(np.float32)


def kernel(**inputs):
    return run(inputs)
```

```python
import numpy as np
from contextlib import ExitStack
import concourse.bass as bass
import concourse.mybir as mybir
from concourse.bass_utils import run_bass_kernel_spmd

F32 = mybir.dt.float32
BF16 = mybir.dt.bfloat16
AF = mybir.ActivationFunctionType
ALU = mybir.AluOpType

D_MODEL = 1024
D_IN = 3256
D_FF = 2816
NFF = D_FF // 128
TT = 256
NCH = TT // 64
NBK = TT // 128
EPS = 1e-6
NPP = 166


class Sched:
    ENGS = ("pe", "act", "dve", "pool", "sp")

    def __init__(self, nc):
        self.nc = nc
        self.ops = {e: [] for e in self.ENGS}
        self.last_w = {}
        self.readers = {}
        self.dma_cnt = {}
        self.seen = {e: {} for e in self.ENGS}
        self.ecnt = {}

    def _need(self, eng, tok, waits, raw):
        if tok is None:
            return
        sem, val, teng, tidx = tok
        if teng == eng:
            if eng == "pe" or not raw:
                return
            if tidx != len(self.ops[eng]) - 1:
                return
        if self.seen[eng].get(sem, 0) >= val:
            return
        self.seen[eng][sem] = val
        waits.append((sem, val))

    def add(self, eng, fn, reads=(), writes=(), dma_sem=None):
        waits = []
        for k in reads:
            self._need(eng, self.last_w.get(k), waits, True)
        for k in writes:
            self._need(eng, self.last_w.get(k), waits, False)
            for r in self.readers.get(k, {}).values():
                self._need(eng, r, waits, False)
        idx = len(self.ops[eng])
        if dma_sem is None:
            self.ecnt[eng] = self.ecnt.get(eng, 0) + 1
            tok = ("e_" + eng, self.ecnt[eng], eng, idx)
            rk = eng
        else:
            c = self.dma_cnt.get(dma_sem, 0) + 1
            self.dma_cnt[dma_sem] = c
            tok = (dma_sem, 16 * c, None, idx)
            rk = dma_sem
        self.ops[eng].append((fn, waits, dma_sem))
        for k in reads:
            self.readers.setdefault(k, {})[rk] = tok
        for k in writes:
            self.last_w[k] = tok
            self.readers[k] = {}
        return tok

    def emit(self, final_reads=()):
        nc = self.nc
        for e in ("sp", "pool"):
            self.add(e, lambda en: en.nop(), reads=final_reads)
        with ExitStack() as st:
            sems = {}
            for e in self.ENGS:
                sems["e_" + e] = st.enter_context(nc.semaphore("e_" + e))
            for name in self.dma_cnt:
                sems[name] = st.enter_context(nc.semaphore(name))
            with nc.Block() as block:
                def mk(engname):
                    def run(e):
                        for fn, waits, dma_sem in self.ops[engname]:
                            for s, v in waits:
                                e.wait_ge(sems[s], v)
                            ins = fn(e)
                            if dma_sem is None:
                                ins.then_inc(sems["e_" + engname], 1)
                            else:
                                ins.then_inc(sems[dma_sem], 16)
                    return run
                block.tensor(mk("pe"))
                block.scalar(mk("act"))
                block.vector(mk("dve"))
                block.gpsimd(mk("pool"))
                block.sync(mk("sp"))


class _Stop(Exception):
    pass


def build(S, DEPTH, stop=None):
    NT = S // TT
    NKB = S // 128
    nc = bass.Bass("TRN2", target_bir_lowering=False)
    dI = lambda n, s: nc.dram_tensor(n, s, F32, kind="ExternalInput").ap()
    xT = dI("xT", [D_MODEL, S])
    w_in = dI("w_in", [DEPTH, D_MODEL, D_IN])
    w_out = dI("w_out", [DEPTH, 1024, 1024])
    w_gate = dI("ffn_w_gate", [DEPTH, 1024, D_FF])
    w_up = dI("ffn_w_up", [DEPTH, 1024, D_FF])
    w_down = dI("ffn_w_down", [DEPTH, D_FF, 1024])
    w_uq = dI("mla_w_uq", [DEPTH, 256, 384])
    w_ukv = dI("mla_w_ukv", [DEPTH, 128, 512])
    w_gk2 = dI("gla_w_gk2", [DEPTH, 16, 128])
    pp_d = dI("pp", [128, DEPTH, NPP])
    tabq = dI("tabq", [96, 2, S])
    tabk = dI("tabk", [32, 2, S])
    outT = nc.dram_tensor("outT", [D_MODEL, S], F32, kind="ExternalOutput").ap()
    xs = [nc.dram_tensor(f"xs{i}", [D_MODEL, S], F32, kind="Internal").ap() for i in range(2)]
    kd = [nc.dram_tensor(f"kd{h}", [96, S], BF16, kind="Internal").ap() for h in range(4)]

    pieces = {}
    plist = []
    for l in range(DEPTH):
        wi = w_in[l].rearrange("(c p) n -> p c n", p=128)
        for nm, lo, hi in (("A0", 0, 512), ("A1", 512, 1024), ("B0", 1024, 1536), ("B1", 1536, 1808), ("C", 1808, 2224),
                           ("D0", 2224, 2736), ("D1", 2736, 2992), ("D2", 2992, 3256)):
            plist.append((f"{nm}_{l}", wi[:, :, lo:hi], 128, 8, hi - lo))
        wo = w_out[l].rearrange("(c p) n -> p c n", p=64)
        for j in range(4):
            plist.append((f"O{j}_{l}", wo[:, :, j * 256:(j + 1) * 256], 64, 16, 256))
        wg = w_gate[l].rearrange("(c p) n -> p c n", p=128)
        wu = w_up[l].rearrange("(c p) n -> p c n", p=128)
        for j in range(6):
            hi = min(D_FF, (j + 1) * 512)
            plist.append((f"G{j}_{l}", wg[:, :, j * 512:hi], 128, 8, hi - j * 512))
            plist.append((f"U{j}_{l}", wu[:, :, j * 512:hi], 128, 8, hi - j * 512))
        wd = w_down[l].rearrange("(c p) n -> p c n", p=128)
        for j in range(8):
            plist.append((f"W{j}_{l}", wd[:, :, j * 128:(j + 1) * 128], 128, NFF, 128))
    for i, (nm, src, P, kc, cols) in enumerate(plist):
        pieces[nm] = (i, P, kc, cols)
    wsc = nc.dram_tensor("wsc", [len(plist), 128, 4096], BF16, kind="Internal").ap()

    Sd = Sched(nc)
    BANK = {"pd0": 0, "pd1": 1, "pq0": 6, "pq1": 0, "pq2": 1, "pq3": 5, "pss": 5, "psa0": 2, "psa1": 3, "po": 4,
            "pss2": 4, "ptb0": 7, "ptb1": 7}

    def add(eng, fn, reads=(), writes=(), dma_sem=None):
        reads, writes = list(reads), list(writes)
        if eng == "pe":
            writes += [f"BK{BANK[k]}" for k in writes if k in BANK]
        elif eng == "dve":
            reads += [f"BK{BANK[k]}" for k in reads if k in BANK]
        return Sd.add(eng, fn, reads, writes, dma_sem)

    with ExitStack() as st:
        def sb(n, s, d=F32):
            return st.enter_context(nc.sbuf_tensor(n, s, d))

        def psf(n):
            return st.enter_context(nc.psum_tensor(n, [128, 256], F32))

        xt = sb("xt", [128, 8, TT])
        hb = sb("hb", [128, 8, TT], BF16)
        yb = sb("yb", [128, 8, TT])
        sqb = sb("sqb", [128, TT], BF16)
        rstd = sb("rstd", [128, TT])
        NWB = 6
        wbuf = [sb(f"wbuf{j}", [128, 4096], BF16) for j in range(NWB)]
        stg32 = [sb(f"stg32_{j}", [128, 2048]) for j in range(2)]
        pp = sb("pp_sb", [128, DEPTH, NPP])
        ones = sb("ones", [128, 128], BF16)
        ident = sb("ident", [128, 128], BF16)
        identf = sb("identf", [128, 128])
        seg = sb("seg", [128, TT])
        mU = sb("mU", [64, NCH, 64])
        mUs = sb("mUs", [64, NCH, 64])
        mLs = sb("mLs", [64, NCH, 64])
        tri = sb("tri", [128, 128], BF16)
        sel = sb("sel", [4, 4, 64])
        Esel = sb("Esel", [65, 64])
        rotq = sb("rotq", [96, 96], BF16)
        rotk = sb("rotk", [32, 32], BF16)
        padI = sb("padI", [32, 96], BF16)
        mix = sb("mix", [64, 16, TT], BF16)
        hid = sb("hid", [128, NFF, TT], BF16)
        gtail = sb("gtail", [128, NFF, 2])
        graw = sb("graw", [128, TT + 2])
        NTMP = 8
        tf = [sb(f"tf{i}", [128, TT]) for i in range(NTMP)]
        tb = [sb(f"tb{i}", [128, TT], BF16) for i in range(NTMP)]
        Sa32 = [sb(f"Sa32_{h}", [64, 64]) for h in range(4)]
        Sb32 = [sb(f"Sb32_{h}", [32, 64]) for h in range(4)]
        Sd32 = [sb(f"Sd32_{h}", [64, 64]) for h in range(4)]
        Sbf = sb("Sbf", [64, NCH, 64], BF16)
        Sdbf = [sb(f"Sdbf_{h}", [64, 64], BF16) for h in range(4)]
        vtok = sb("vtok", [64, NCH, 256], BF16)
        k2tok = sb("k2tok", [64, NCH, 64], BF16)
        attT = sb("attT", [64, NCH, 64], BF16)
        acol = sb("acol", [64, NCH])
        blast = sb("blast", [64, NCH])
        codeb = sb("codeb", [16, TT], BF16)
        wgk = sb("wgk", [16, 128], BF16)
        wuq = sb("wuq", [128, 2, 384], BF16)
        wkpad = sb("wkpad", [128, 4, 96], BF16)
        wv = sb("wv", [128, 4, 64], BF16)
        cqf = sb("cqf", [128, 2, TT])
        cqn = sb("cqn", [128, 2, TT], BF16)
        ckvn = sb("ckvn", [128, TT], BF16)
        krope = sb("krope", [32, TT], BF16)
        qTh = [sb(f"qTh{h}", [96, TT], BF16) for h in range(4)]
        ksb = [sb(f"ksb{i}", [96, TT], BF16) for i in range(2)]
        KP = 1024
        kbuf = [sb(f"kbuf{i}", [96, KP], BF16) for i in range(2)]
        vc = sb("vc", [128, NKB, 260], BF16)
        tq = sb("tq", [96, 2, TT])
        tk = sb("tk", [32, 2, TT])
        pT = [sb(f"pT{i}", [128, TT], BF16) for i in range(2)]
        ob = sb("ob", [65, TT])
        tfa = sb("tfa", [64, TT])
        graw2 = sb("graw2", [128, TT + 2])
        ctail = sb("ctail", [64, 12, 3])
        cbuf = sb("cbuf", [64, TT + 3])
        dq = [sb(f"dq{i}", [64, TT]) for i in range(3)]
        bet4 = sb("bet4", [4, TT])
        lg4 = sb("lg4", [4, TT])
        b4 = sb("b4", [4, TT])
        negA = sb("negA", [4, DEPTH])
        btok = sb("btok", [64, NCH, 4])
        lmT = sb("lmT", [64, NCH, 64])
        lm = sb("lm", [64, NCH, 64])
        Zb = [sb(f"Zb{i}", [64, NCH, 64], BF16) for i in range(2)]
        Yb = [sb(f"Yb{i}", [64, NCH, 64], BF16) for i in range(2)]
        P32 = sb("P32", [64, NCH, 64])
        Pb = sb("Pb", [64, NCH, 64], BF16)
        aqk = sb("aqk", [64, NCH, 64], BF16)
        kbf = sb("kbf", [64, TT], BF16)
        kbb = sb("kbb", [64, TT], BF16)
        qbf = sb("qbf", [64, TT], BF16)
        q2b = sb("q2b", [64, TT], BF16)
        vbtok = sb("vbtok", [64, NCH, 64], BF16)
        kbetok = sb("kbetok", [64, NCH, 64], BF16)
        ud = sb("ud", [64, NCH, 64])
        wTd = sb("wTd", [64, NCH, 64], BF16)
        vnew = sb("vnew", [64, 64], BF16)
        elast = sb("elast", [64, NCH])
        lbt = sb("lbt", [64, 4])
        oml = sb("oml", [64, 4])
        noml = sb("noml", [64, 4])

        banks = [st.enter_context(nc.psum_tensor(f"bank{i}", [128, 512], F32)) for i in range(7)]
        h0 = lambda i: banks[i][:, 0:256]
        h1 = lambda i: banks[i][:, 256:512]
        bankb = st.enter_context(nc.psum_tensor("bankb", [128, 1024], BF16))
        ptb = [bankb[:, 0:512], bankb[:, 512:1024]]
        pd = [h0(0), h0(1)]
        pq = [h0(6), h1(0), h1(1), h1(5)]
        pss = h0(5)
        psa = [h0(2), h0(3)]
        po = h0(4)
        pss2 = h1(4)

        def mm(out, lhsT, rhs, r, w, start=True, stop=True):
            add("pe", lambda e: e.matmul(out, lhsT=lhsT, rhs=rhs, start=start, stop=stop), r, w)

        def tr(out, in_, idn, r, w):
            add("pe", lambda e: e.transpose(out, in_, idn), r, w)

        def act(out, in_, func, r, w, bias=None, scale=None):
            kw = {}
            if bias is not None:
                kw["bias"] = bias
            if scale is not None:
                kw["scale"] = scale
            add("act", lambda e: e.activation(out=out, in_=in_, func=func, **kw), r, w)

        def tt(eng, out, a, b, op, r, w):
            add(eng, lambda e: e.tensor_tensor(out=out, in0=a, in1=b, op=op), r, w)

        def ts(eng, out, a, s1, op0, r, w, s2=None, op1=None):
            if op1 is None:
                add(eng, lambda e: e.tensor_scalar(out=out, in0=a, scalar1=s1, scalar2=None, op0=op0), r, w)
            else:
                add(eng, lambda e: e.tensor_scalar(out=out, in0=a, scalar1=s1, scalar2=s2, op0=op0, op1=op1), r, w)

        def stt(eng, out, a, s, b, op0, op1, r, w):
            add(eng, lambda e: e.scalar_tensor_tensor(out=out, in0=a, scalar=s, in1=b, op0=op0, op1=op1), r, w)

        def cp(eng, out, in_, r, w):
            if eng == "act":
                act(out, in_, AF.Copy, r, w)
            else:
                add(eng, lambda e: e.tensor_copy(out=out, in_=in_), r, w)

        def ms(eng, ap, v, w):
            add(eng, lambda e: e.memset(ap, v), (), w)

        def asel(out, pattern, op, base, cm, key):
            add("pool", lambda e: e.affine_select(out=out, in_=out, pattern=pattern, compare_op=op, fill=0.0,
                                                  base=base, channel_multiplier=cm), [key], [key])

        def dma(eng, out, in_, r, w, sem):
            add(eng, lambda e: e.dma_start(out=out, in_=in_), r, w, dma_sem=sem)

        def rs_from_ss(ps_ap, n_feat, P):
            act(rstd[:P, :], ps_ap, AF.Sqrt, ["pss"], ["rstd"], bias=EPS, scale=1.0 / n_feat)
            add("dve", lambda e: e.reciprocal(out=rstd[:P, :], in_=rstd[:P, :]), ["rstd"], ["rstd"])

        def ck(name, l, n, tsl, dst, dkey):
            if stop == name:
                dma("pool", dst.rearrange("(c p) t -> p c t", p=128)[:, :, tsl], xt[:], ["xt"], [(dkey, n)], "dout")
                raise _Stop()
        try:
          if stop == "const0":
              ms("pool", xt[:], 0.0, ["xt"])
          ck("const0", 0, 0, slice(0, TT), outT, "outT")
          ms("pool", ones[:], 1.0, ["ones"])
          ms("pool", ident[:], 1.0, ["ident"])
          asel(ident[:], [[-1, 128]], ALU.is_equal, 0, 1, "ident")
          ms("pool", identf[:], 1.0, ["identf"])
          asel(identf[:], [[-1, 128]], ALU.is_equal, 0, 1, "identf")
          ms("pool", seg[:], 1.0, ["seg"])
          ms("pool", seg[:].rearrange("p (c t) -> p c t", t=64)[:, :, 0:1], 0.0, ["seg"])
          if stop == "const1":
              ms("pool", xt[:], 0.0, ["xt"])
          ck("const1", 0, 0, slice(0, TT), outT, "outT")
          ms("pool", mU[:], 1.0, ["mU"])
          asel(mU[:], [[0, NCH], [1, 64]], ALU.is_ge, 0, -1, "mU")
          ms("pool", mUs[:], -1.0, ["mUs"])
          asel(mUs[:], [[0, NCH], [1, 64]], ALU.is_gt, 0, -1, "mUs")
          ms("pool", mLs[:], -1.0, ["mLs"])
          asel(mLs[:], [[0, NCH], [-1, 64]], ALU.is_gt, 0, 1, "mLs")
          ms("pool", tri[:], 1.0, ["tri"])
          asel(tri[:], [[1, 128]], ALU.is_ge, 0, -1, "tri")
          ms("pool", sel[:], 1.0, ["sel"])
          asel(sel[:], [[-1, 4], [0, 64]], ALU.is_equal, 0, 1, "sel")
          ms("pool", Esel[:], 1.0, ["Esel"])
          asel(Esel[:], [[0, 64]], ALU.is_ge, -64, 1, "Esel")
          if stop == "const2":
              ms("pool", xt[:], 0.0, ["xt"])
          ck("const2", 0, 0, slice(0, TT), outT, "outT")
          for nm, t_, n, r0 in (("rotq", rotq, 96, 64), ("rotk", rotk, 32, 0)):
              ms("pool", t_[:], 0.0, [nm])
              ms("pool", t_[:, r0:r0 + 16], -1.0, [nm])
              asel(t_[:, r0:r0 + 16], [[-1, 16]], ALU.is_equal, -16 - r0, 1, nm)
              ms("pool", t_[:, r0 + 16:r0 + 32], 1.0, [nm])
              asel(t_[:, r0 + 16:r0 + 32], [[-1, 16]], ALU.is_equal, -r0, 1, nm)
          ms("pool", padI[:], 0.0, ["padI"])
          ms("pool", padI[:, 64:96], 1.0, ["padI"])
          asel(padI[:, 64:96], [[-1, 32]], ALU.is_equal, 0, 1, "padI")
          if stop == "const3":
              ms("pool", xt[:], 0.0, ["xt"])
          ck("const3", 0, 0, slice(0, TT), outT, "outT")
          ms("pool", vc[:], 1.0, ["vc"])
          if stop == "const4":
              ms("pool", xt[:], 0.0, ["xt"])
          ck("const4", 0, 0, slice(0, TT), outT, "outT")
          dma("sp", pp[:], pp_d[:, :, :], [], ["pp"], "dpp")
          if stop == "const":
              ms("pool", xt[:], 0.0, ["xt"])
          ck("const", 0, 0, slice(0, TT), outT, "outT")

          cast_eng = ["dve", "pool", "dve"]
          ci_ = 0
          for i, (nm, src, P, kcn, cols) in enumerate(plist):
              hk = kcn // 2
              for half in range(2):
                  b = ci_ % 2
                  n = hk * cols
                  dma("sp", stg32[b][:P, :n].rearrange("p (c n) -> p c n", n=cols), src[:, half * hk:(half + 1) * hk, :], [],
                      [f"stg32{b}"], f"pl{b}")
                  cp(cast_eng[ci_ % 3], wbuf[b][:P, :n], stg32[b][:P, :n], [f"stg32{b}"], [f"wb{b}"])
                  dma("sp", wsc[i, :P, half * n:(half + 1) * n], wbuf[b][:P, :n], [f"wb{b}"], [("wsc", nm)], f"ps{b}")
                  ci_ += 1

          if stop == "prologue":
              ms("pool", xt[:], 0.0, ["xt"])
          ck("prologue", 0, 0, slice(0, TT), outT, "outT")
          NT_ = S // TT
          tile_order = lambda l: ([f"{nm}_{l}" for nm in ("C", "A0", "A1", "B0", "B1", "D0", "D1", "D2", "O0", "O1", "O2", "O3")]
                                  + [f"{g}{j}_{l}" for j in range(6) for g in ("G", "U")] + [f"W{j}_{l}" for j in range(8)])
          gorder = [(l, n, nm) for l in range(DEPTH) for n in range(NT_) for nm in tile_order(l)]
          gindex = {inst: i for i, inst in enumerate(gorder)}
          wstate = {"nxt": 0, "owner": [None] * NWB, "map": {}, "cur": (0, 0)}

          def wload(inst, j):
              i, P, kcn, cols = pieces[inst[2]]
              wstate["owner"][j] = inst
              wstate["map"][inst] = j
              dma("sp", wbuf[j][:P, :kcn * cols], wsc[i, :P, :kcn * cols], [("wsc", inst[2])], [f"wb{j}"], f"wl{j}")

          def need(*names):
              live = [wstate["cur"] + (nm,) for nm in names]
              lo = min(gindex[i] for i in live)
              hi = max(gindex[i] for i in live)

              def free_buf():
                  for j in range(NWB):
                      o = wstate["owner"][j]
                      if o is None or gindex[o] < lo:
                          return j
                  return None
              while wstate["nxt"] < len(gorder):
                  j = free_buf()
                  if j is None:
                      assert wstate["nxt"] > hi, "weight buffers exhausted"
                      break
                  wload(gorder[wstate["nxt"]], j)
                  wstate["nxt"] += 1
                  if wstate["nxt"] > hi + NWB:
                      break

          def wview(nm):
              inst = wstate["cur"] + (nm,)
              i, P, kcn, cols = pieces[nm]
              j = wstate["map"][inst]
              assert wstate["owner"][j] == inst
              return wbuf[j][:P, :kcn * cols].rearrange("p (c n) -> p c n", n=cols), f"wb{j}"

          def lin_fm(nm, lo, hi, rhs_fn, nk, out_ps, okey, rkeys):
              wvw, wk = wview(nm)
              for k in range(nk):
                  mm(out_ps, wvw[:, k, lo:hi], rhs_fn(k), [wk] + rkeys, [okey], start=(k == 0), stop=(k == nk - 1))

          def rmsnorm_x(gcol, l):
              for c in range(8):
                  act(sqb[:], xt[:, c, :], AF.Square, ["xt"], ["sqb"])
                  mm(pss[:], ones[:], sqb[:], ["ones", "sqb"], ["pss"], start=(c == 0), stop=(c == 7))
              rs_from_ss(pss[:], 1024.0, 128)
              for c in range(8):
                  stt("dve", hb[:, c, :], xt[:, c, :], pp[:, l, gcol + c:gcol + c + 1], rstd[:],
                      ALU.mult, ALU.mult, ["xt", "pp", "rstd"], ["hb"])

          def post_norm_residual(gcol, l):
              for c in range(8):
                  act(sqb[:], yb[:, c, :], AF.Square, ["yb"], ["sqb"])
                  mm(pss[:], ones[:], sqb[:], ["ones", "sqb"], ["pss"], start=(c == 0), stop=(c == 7))
              rs_from_ss(pss[:], 1024.0, 128)
              for c in range(8):
                  stt("dve", yb[:, c, :], yb[:, c, :], pp[:, l, gcol + c:gcol + c + 1], rstd[:], ALU.mult, ALU.mult,
                      ["yb", "pp", "rstd"], ["yb"])
                  tt("pool", xt[:, c, :], xt[:, c, :], yb[:, c, :], ALU.add, ["xt", "yb"], ["xt"])

          def head_out(po_ap, gate_ap, gkeys, ngcol, l, mi):
              act(tb[7][:64, :], po_ap, AF.Square, ["pq3"], ["tb7"])
              mm(pss[:64, :], ones[:64, :64], tb[7][:64, :], ["ones", "tb7"], ["pss"])
              rs_from_ss(pss[:64, :], 64.0, 64)
              stt("dve", tf[7][:64, :], po_ap, pp[:64, l, ngcol:ngcol + 1], rstd[:64, :], ALU.mult, ALU.mult,
                  ["pq3", "pp", "rstd"], ["tf7"])
              tt("dve", mix[:, mi, :], tf[7][:64, :], gate_ap, ALU.mult, ["tf7"] + gkeys, ["mix"])
              tk_["f"]()

          hbr = lambda k: hb[:, k, :]

          tk_ = {"f": lambda: None}

          def chunk_gla(dk, q_ap, k_ap, g_ap, keys, gscale, qscale, S32, hcol):
              b_ = tf[3][:dk, :]
              add("dve", lambda e: e.tensor_tensor_scan(out=b_, data0=seg[:dk, :], data1=g_ap, initial=0.0,
                                                        op0=ALU.mult, op1=ALU.add), ["seg"] + keys, ["tf3"])
              b3 = b_.rearrange("p (c t) -> p c t", t=64)
              act(tf[4][:dk, :], b_, AF.Exp, ["tf3"], ["tf4"], scale=gscale)
              ts("dve", tf[5][:dk, :], b_, -gscale, ALU.mult, ["tf3"], ["tf5"], s2=80.0, op1=ALU.min)
              cp("dve", blast[:dk, :], b3[:, :, 63], ["tf3"], ["blast"])
              tt("dve", tf[6][:dk, :].rearrange("p (c t) -> p c t", t=64), blast[:dk, :].unsqueeze(2).to_broadcast([dk, NCH, 64]),
                 b3, ALU.subtract, ["blast", "tf3"], ["tf6"])
              act(tf[5][:dk, :], tf[5][:dk, :], AF.Exp, ["tf5"], ["tf5"])
              act(tf[6][:dk, :], tf[6][:dk, :], AF.Exp, ["tf6"], ["tf6"], scale=gscale)
              act(acol[:dk, :], blast[:dk, :], AF.Exp, ["blast"], ["acol"], scale=gscale)
              stt("dve", tb[0][:dk, :], q_ap, qscale, tf[4][:dk, :], ALU.mult, ALU.mult, keys + ["tf4"], ["tb0"])
              tt("dve", tb[1][:dk, :], k_ap, tf[5][:dk, :], ALU.mult, keys + ["tf5"], ["tb1"])
              tt("dve", tb[2][:dk, :], k_ap, tf[6][:dk, :], ALU.mult, keys + ["tf6"], ["tb2"])
              tk_["f"]()
              for c in range(NCH):
                  tr(ptb[0][:64, c * dk:(c + 1) * dk], tb[2][:dk, c * 64:(c + 1) * 64], ident[:dk, :dk], ["tb2", "ident"], ["ptb0"])
              cp("act", k2tok[:, :, :dk], ptb[0][:64, :NCH * dk].rearrange("p (c d) -> p c d", d=dk), ["ptb0"], ["k2tok"])
              tk_["f"]()
              for c in range(NCH):
                  mm(pq[0][:64, c * 64:(c + 1) * 64], tb[1][:dk, c * 64:(c + 1) * 64], tb[0][:dk, c * 64:(c + 1) * 64],
                     ["tb0", "tb1"], ["pq0"])
              tt("dve", attT[:], pq[0][:64, :].rearrange("p (c t) -> p c t", t=64), mU[:], ALU.mult, ["pq0", "mU"], ["attT"])
              tk_["f"]()
              for c in range(NCH):
                  mm(pq[1][:dk, c * 64:(c + 1) * 64], k2tok[:, c, :dk], vtok[:, c, hcol:hcol + 64], ["k2tok", "vtok"], ["pq1"])
              for c in range(NCH):
                  cp("dve", Sbf[:dk, c, :], S32[:], [S32.name], ["Sbf"])
                  stt("dve", S32[:], S32[:], acol[:dk, c:c + 1], pq[1][:dk, c * 64:(c + 1) * 64], ALU.mult, ALU.add,
                      [S32.name, "acol", "pq1"], [S32.name])
                  tk_["f"]()
              for c in range(NCH):
                  o_ = pq[3][:64, c * 64:(c + 1) * 64]
                  mm(o_, vtok[:, c, hcol:hcol + 64], attT[:, c, :], ["vtok", "attT"], ["pq3"], start=True, stop=False)
                  mm(o_, Sbf[:dk, c, :], tb[0][:dk, c * 64:(c + 1) * 64], ["Sbf", "tb0"], ["pq3"], start=False, stop=True)

          def rope(ps_ap, R, tab, tkey, rot, rkey, out_ap, okey, pkey):
              cp("act", tf[6][:R, :], ps_ap, [pkey], ["tf6"])
              cp("pool", tb[6][:R, :], tf[6][:R, :], ["tf6"], ["tb6"])
              mm(pq[0][:R, :], rot[:R, :R], tb[6][:R, :], [rkey, "tb6"], ["pq0"])
              tt("dve", tf[5][:R, :], pq[0][:R, :], tab[:, 1, :], ALU.mult, ["pq0", tkey], ["tf5"])
              tt("dve", tf[6][:R, :], tf[6][:R, :], tab[:, 0, :], ALU.mult, ["tf6", tkey], ["tf6"])
              tt("pool", out_ap, tf[6][:R, :], tf[5][:R, :], ALU.add, ["tf5", "tf6"], [okey])

          for l in range(DEPTH):
              src = xT if l == 0 else xs[(l - 1) % 2]
              dst = outT if l == DEPTH - 1 else xs[l % 2]
              skey = "xT" if l == 0 else f"xs{(l - 1) % 2}"
              dkey = "outT" if l == DEPTH - 1 else f"xs{l % 2}"
              for t_ in Sa32 + Sb32 + Sd32:
                  ms("pool", t_[:], 0.0, [t_.name])
              ms("pool", ctail[:], 0.0, ["ctail"])
              ms("pool", gtail[:], 0.0, ["gtail"])
              dma("sp", stg32[0][:, :768].rearrange("p (c n) -> p c n", n=384), w_uq[l].rearrange("(c p) n -> p c n", p=128),
                  [], ["stg320"], "dsm")
              cp("dve", wuq[:], stg32[0][:, :768].rearrange("p (c n) -> p c n", n=384), ["stg320"], ["wuq"])
              dma("sp", stg32[1][:, :512], w_ukv[l], [], ["stg321"], "dsm2")
              ms("pool", wkpad[:], 0.0, ["wkpad"])
              s4 = stg32[1][:, :512].rearrange("p (h n) -> p h n", n=128)
              cp("dve", wkpad[:, :, 0:64], s4[:, :, 0:64], ["stg321"], ["wkpad"])
              cp("dve", wv[:], s4[:, :, 64:128], ["stg321"], ["wv"])
              dma("sp", stg32[0][:16, 1024:1152], w_gk2[l], [], ["stg320"], "dsm")
              cp("dve", wgk[:], stg32[0][:16, 1024:1152], ["stg320"], ["wgk"])
              if l == 0:
                  ms("pool", lbt[:], 0.0, ["lbt"])
              else:
                  tt("dve", lbt[:], pp[:64, l, 36:40], pp[:64, l, 32:36], ALU.subtract, ["pp"], ["lbt"])
                  act(lbt[:], lbt[:], AF.Sigmoid, ["lbt"], ["lbt"])
              ts("dve", oml[:], lbt[:], -1.0, ALU.mult, ["lbt"], ["oml"], s2=1.0, op1=ALU.add)
              ts("dve", noml[:], oml[:], -1.0, ALU.mult, ["oml"], ["noml"])
              act(negA[:, l:l + 1], pp[:4, l, 98:99], AF.Exp, ["pp"], ["negA"])
              ts("dve", negA[:, l:l + 1], negA[:, l:l + 1], -1.0, ALU.mult, ["negA"], ["negA"])

              for n in range(NT):
                  t0 = n * TT
                  tsl = slice(t0, t0 + TT)
                  wstate["cur"] = (l, n)
                  if stop == "setup":
                      ms("pool", xt[:], 0.0, ["xt"])
                  ck("setup", l, n, tsl, dst, dkey)
                  dma("sp", xt[:], src.rearrange("(c p) t -> p c t", p=128)[:, :, tsl], [(skey, n)], ["xt"], "dx")
                  dma("sp", tq[:], tabq[:, :, tsl], [], ["tq"], "dtq")
                  dma("sp", tk[:], tabk[:, :, tsl], [], ["tk"], "dtk")
                  ck("load", l, n, tsl, dst, dkey)
                  rmsnorm_x(0, l)

                  need(f"C_{l}")
                  for c in range(2):
                      lin_fm(f"C_{l}", c * 128, c * 128 + 128, hbr, 8, pd[c][:, :], f"pd{c}", ["hb"])
                      cp("act", cqf[:, c, :], pd[c][:, :], [f"pd{c}"], ["cqf"])
                      act(sqb[:], pd[c][:, :], AF.Square, [f"pd{c}"], ["sqb"])
                      mm(pss[:], ones[:], sqb[:], ["ones", "sqb"], ["pss"], start=(c == 0), stop=(c == 1))
                  rs_from_ss(pss[:], 256.0, 128)
                  for c in range(2):
                      stt("dve", cqn[:, c, :], cqf[:, c, :], pp[:, l, 47 + c:48 + c], rstd[:], ALU.mult, ALU.mult,
                          ["cqf", "pp", "rstd"], ["cqn"])
                  lin_fm(f"C_{l}", 256, 384, hbr, 8, pd[0][:, :], "pd0", ["hb"])
                  act(sqb[:], pd[0][:, :], AF.Square, ["pd0"], ["sqb"])
                  mm(pss[:], ones[:], sqb[:], ["ones", "sqb"], ["pss"])
                  rs_from_ss(pss[:], 128.0, 128)
                  stt("dve", ckvn[:], pd[0][:, :], pp[:, l, 49:50], rstd[:], ALU.mult, ALU.mult, ["pd0", "pp", "rstd"], ["ckvn"])
                  lin_fm(f"C_{l}", 384, 416, hbr, 8, pd[1][:32, :], "pd1", ["hb"])
                  rope(pd[1][:32, :], 32, tk, "tk", rotk, "rotk", krope[:], "krope", "pd1")
                  ck("C1", l, n, tsl, dst, dkey)
                  for h in range(4):
                      mm(pd[h % 2][:96, :], wkpad[:, h, :], ckvn[:], ["wkpad", "ckvn"], [f"pd{h % 2}"], start=True, stop=False)
                      mm(pd[h % 2][:96, :], padI[:], krope[:], ["padI", "krope"], [f"pd{h % 2}"], start=False, stop=True)
                      cp("act", ksb[h % 2][:], pd[h % 2][:96, :], [f"pd{h % 2}"], [f"ksb{h % 2}"])
                      dma("pool", kd[h][:, tsl], ksb[h % 2][:], [f"ksb{h % 2}"], [f"kd{h}"], f"kst{h % 2}")
                  ck("C2", l, n, tsl, dst, dkey)
                  for m in range(NBK):
                      mm(pd[m % 2][:, :], ckvn[:, m * 128:(m + 1) * 128], wv[:].rearrange("p h n -> p (h n)"), ["ckvn", "wv"],
                         [f"pd{m % 2}"])
                      cp("act", vc[:, n * NBK + m, :].rearrange("p (h n) -> p h n", n=65)[:, :, 0:64],
                         pd[m % 2][:, :].rearrange("p (h n) -> p h n", n=64), [f"pd{m % 2}"], ["vc"])
                  ck("C3", l, n, tsl, dst, dkey)
                  for h in range(4):
                      for c in range(2):
                          mm(pd[h % 2][:96, :], wuq[:, c, h * 96:(h + 1) * 96], cqn[:, c, :], ["wuq", "cqn"], [f"pd{h % 2}"],
                             start=(c == 0), stop=(c == 1))
                      ck(f"Q{h}a", l, n, tsl, dst, dkey)
                      rope(pd[h % 2][:96, :], 96, tq, "tq", rotq, "rotq", qTh[h][:], f"qTh{h}", f"pd{h % 2}")
                      ck(f"Q{h}b", l, n, tsl, dst, dkey)
                  ck("C4", l, n, tsl, dst, dkey)
                  scale = 96.0 ** -0.5
                  nkb = (n + 1) * NBK

                  def pv(o_, l_, r_, s_, e_, pk):
                      add("pe", lambda e: e.matmul(o_, lhsT=l_, rhs=r_, start=s_, stop=e_, skip_group_check=True), ["vc", pk],
                          ["po", "pss2"] if s_ else ["po"])

                  def att_gen():
                      ai = 0
                      ki = 0
                      for h in range(4):
                          for kb in range(nkb):
                              if kb % (KP // 128) == 0:
                                  kbf_ = kbuf[ki % 2]
                                  kk = f"kbuf{ki % 2}"
                                  ki += 1
                                  nk_ = min(KP, (nkb - kb) * 128)
                                  dma("sp", kbf_[:, :nk_], kd[h][:, kb * 128:kb * 128 + nk_], [f"kd{h}"], [kk], "kl" + kk[-1])
                              ko = (kb % (KP // 128)) * 128
                              m = kb - n * NBK
                              qlo = max(m, 0) * 128
                              sa = psa[ai % 2]
                              sk = f"psa{ai % 2}"
                              pt_ = pT[ai % 2]
                              pk = f"pT{ai % 2}"
                              ai += 1
                              mm(sa[:, qlo:TT], kbf_[:, ko:ko + 128], qTh[h][:, qlo:TT], [kk, f"qTh{h}"], [sk])
                              act(pt_[:, qlo:TT], sa[:, qlo:TT], AF.Exp, [sk], [pk], scale=scale)
                              if m >= 0:
                                  tt("pool", pt_[:, qlo:qlo + 128], pt_[:, qlo:qlo + 128], tri[:], ALU.mult, [pk, "tri"], [pk])
                              pv(po[:65, qlo:TT], vc[:, kb, h * 65:(h + 1) * 65], pt_[:, qlo:TT], kb == 0, kb == nkb - 1, pk)
                              yield
                          cp("act", ob[:], po[:65, :], ["po"], ["ob"])
                          mm(pss2[:64, :], Esel[:], ob[:], ["Esel", "ob"], ["pss2"])
                          add("dve", lambda e: e.reciprocal(out=tfa[:], in_=pss2[:64, :]), ["pss2"], ["tfa"])
                          tt("dve", mix[:, 8 + h, :], ob[:64, :], tfa[:], ALU.mult, ["ob", "tfa"], ["mix"])
                          yield
                  att = att_gen()
                  per_tick = max(1, -(-(4 * nkb + 4) // 110))

                  def tick():
                      for _ in range(per_tick):
                          if next(att, "done") == "done":
                              break
                  tk_["f"] = tick
                  ck("norm", l, n, tsl, dst, dkey)
                  need(f"A0_{l}", f"A1_{l}")
                  wvw, wk = wview(f"A1_{l}")
                  for c in range(NCH):
                      for k in range(8):
                          mm(pd[c % 2][:64, :], hb[:, k, c * 64:(c + 1) * 64], wvw[:, k, 0:256], ["hb", wk], [f"pd{c % 2}"],
                             start=(k == 0), stop=(k == 7))
                      cp("act", vtok[:, c, :], pd[c % 2][:64, :], [f"pd{c % 2}"], ["vtok"])
                  for h in range(4):
                      lin_fm(f"A0_{l}", h * 64, h * 64 + 64, hbr, 8, pd[0][:64, :], "pd0", ["hb"])
                      lin_fm(f"A0_{l}", 256 + h * 64, 256 + h * 64 + 64, hbr, 8, pd[1][:64, :], "pd1", ["hb"])
                      act(tf[0][:64, :], pd[0][:64, :], AF.Silu, ["pd0"], ["tf0"])
                      act(tf[1][:64, :], pd[1][:64, :], AF.Sigmoid, ["pd1"], ["tf1"])
                      ts("dve", tf[2][:64, :], tf[1][:64, :], noml[:, h:h + 1], ALU.mult, ["tf1", "noml", "oml"], ["tf2"],
                         s2=oml[:, h:h + 1], op1=ALU.add)
                      ts("dve", tf[1][:64, :], tf[1][:64, :], oml[:, h:h + 1], ALU.mult, ["tf1", "oml", "lbt"], ["tf1"],
                         s2=lbt[:, h:h + 1], op1=ALU.add)
                      act(tf[1][:64, :], tf[1][:64, :], AF.Ln, ["tf1"], ["tf1"])
                      tk_["f"]()
                      chunk_gla(64, tf[0][:64, :], tf[2][:64, :], tf[1][:64, :], ["tf0", "tf1", "tf2"], 1.0, 0.125, Sa32[h], h * 64)
                      lin_fm(f"A1_{l}", 256 + h * 64, 256 + h * 64 + 64, hbr, 8, pd[0][:64, :], "pd0", ["hb"])
                      act(tf[0][:64, :], pd[0][:64, :], AF.Sigmoid, ["pd0"], ["tf0"])
                      head_out(pq[3][:64, :], tf[0][:64, :], ["tf0"], 40, l, h)

                  ck("A", l, n, tsl, dst, dkey)
                  need(f"B0_{l}", f"B1_{l}")
                  wvw, wk = wview(f"B0_{l}")
                  for c in range(NCH):
                      for k in range(8):
                          mm(pd[c % 2][:64, :], hb[:, k, c * 64:(c + 1) * 64], wvw[:, k, 256:512], ["hb", wk], [f"pd{c % 2}"],
                             start=(k == 0), stop=(k == 7))
                      cp("act", vtok[:, c, :], pd[c % 2][:64, :], [f"pd{c % 2}"], ["vtok"])
                  lin_fm(f"B1_{l}", 0, 16, hbr, 8, pd[0][:16, :], "pd0", ["hb"])
                  cp("act", codeb[:], pd[0][:16, :], ["pd0"], ["codeb"])
                  for h in range(4):
                      lin_fm(f"B0_{l}", h * 32, h * 32 + 32, hbr, 8, pd[0][:32, :], "pd0", ["hb"])
                      lin_fm(f"B0_{l}", 128 + h * 32, 128 + h * 32 + 32, hbr, 8, pd[1][:32, :], "pd1", ["hb"])
                      cp("act", tf[0][:32, :], pd[0][:32, :], ["pd0"], ["tf0"])
                      cp("act", tf[2][:32, :], pd[1][:32, :], ["pd1"], ["tf2"])
                      mm(pd[0][:32, :], wgk[:, h * 32:h * 32 + 32], codeb[:], ["wgk", "codeb"], ["pd0"])
                      act(tf[1][:32, :], pd[0][:32, :], AF.Sigmoid, ["pd0", "pp"], ["tf1"], bias=pp[:32, l, 43 + h:44 + h])
                      act(tf[1][:32, :], tf[1][:32, :], AF.Ln, ["tf1"], ["tf1"])
                      tk_["f"]()
                      chunk_gla(32, tf[0][:32, :], tf[2][:32, :], tf[1][:32, :], ["tf0", "tf1", "tf2"], 1.0 / 16.0, 32.0 ** -0.5,
                                Sb32[h], h * 64)
                      lin_fm(f"B1_{l}", 16 + h * 64, 16 + h * 64 + 64, hbr, 8, pd[0][:64, :], "pd0", ["hb"])
                      act(tf[0][:64, :], pd[0][:64, :], AF.Silu, ["pd0"], ["tf0"])
                      head_out(pq[3][:64, :], tf[0][:64, :], ["tf0"], 41, l, 4 + h)

                  ck("B", l, n, tsl, dst, dkey)
                  ck("C", l, n, tsl, dst, dkey)
                  need(f"D0_{l}", f"D1_{l}", f"D2_{l}")
                  lin_fm(f"D2_{l}", 0, 4, hbr, 8, pd[0][:4, :], "pd0", ["hb"])
                  act(bet4[:], pd[0][:4, :], AF.Sigmoid, ["pd0"], ["bet4"])
                  lin_fm(f"D2_{l}", 4, 8, hbr, 8, pd[1][:4, :], "pd1", ["hb"])
                  act(lg4[:], pd[1][:4, :], AF.Exp, ["pd1", "pp"], ["lg4"], bias=pp[:4, l, 99:100])
                  act(lg4[:], lg4[:], AF.Ln, ["lg4"], ["lg4"], bias=1.0)
                  ts("dve", lg4[:], lg4[:], negA[:, l:l + 1], ALU.mult, ["lg4", "negA"], ["lg4"])
                  add("dve", lambda e: e.tensor_tensor_scan(out=b4[:], data0=seg[:4, :], data1=lg4[:], initial=0.0,
                                                            op0=ALU.mult, op1=ALU.add), ["seg", "lg4"], ["b4"])
                  for c in range(NCH):
                      mm(pq[0][:64, c * 4:(c + 1) * 4], b4[:, c * 64:(c + 1) * 64], identf[:4, :4], ["b4", "identf"], ["pq0"])
                  cp("dve", btok[:], pq[0][:64, :NCH * 4].rearrange("p (c h) -> p c h", h=4), ["pq0"], ["btok"])
                  for h in range(4):
                      mm(pq[2][:64, :], sel[:, h, :], b4[:], ["sel", "b4"], ["pq2"])
                      cp("act", tf[3][:64, :], pq[2][:64, :], ["pq2"], ["tf3"])
                      mm(pq[2][:64, :], sel[:, h, :], bet4[:], ["sel", "bet4"], ["pq2"])
                      cp("act", tf[4][:64, :], pq[2][:64, :], ["pq2"], ["tf4"])
                      bB3 = tf[3][:64, :].rearrange("p (c t) -> p c t", t=64)
                      bt_bc = btok[:, :, h:h + 1].to_broadcast([64, NCH, 64])
                      tt("dve", lmT[:], bB3, bt_bc, ALU.subtract, ["tf3", "btok"], ["lmT"])
                      ts("dve", lmT[:], lmT[:], 0.0, ALU.min, ["lmT"], ["lmT"])
                      act(lmT[:], lmT[:], AF.Exp, ["lmT"], ["lmT"])
                      tt("dve", lm[:], bt_bc, bB3, ALU.subtract, ["tf3", "btok"], ["lm"])
                      ts("dve", lm[:], lm[:], 0.0, ALU.min, ["lm"], ["lm"])
                      act(lm[:], lm[:], AF.Exp, ["lm"], ["lm"])
                      tk_["f"]()
                      for qi, (pn, lo) in enumerate(((f"D0_{l}", h * 64), (f"D0_{l}", 256 + h * 64), (f"D1_{l}", h * 64))):
                          ci = qi * 4 + h
                          lin_fm(pn, lo, lo + 64, hbr, 8, pd[qi % 2][:64, :], f"pd{qi % 2}", ["hb"])
                          cp("pool", cbuf[:, 0:3], ctail[:, ci, :], ["ctail"], ["cbuf"])
                          cp("act", cbuf[:, 3:TT + 3], pd[qi % 2][:64, :], [f"pd{qi % 2}"], ["cbuf"])
                          cw = lambda j: pp[:64, l, 50 + ci * 4 + j:51 + ci * 4 + j]
                          ts("dve", dq[qi][:], cbuf[:, 0:TT], cw(0), ALU.mult, ["cbuf", "pp"], [f"dq{qi}"])
                          for j in (1, 2, 3):
                              stt("dve", dq[qi][:], cbuf[:, j:TT + j], cw(j), dq[qi][:], ALU.mult, ALU.add, ["cbuf", "pp", f"dq{qi}"],
                                  [f"dq{qi}"])
                          cp("pool", ctail[:, ci, :], cbuf[:, TT:TT + 3], ["cbuf"], ["ctail"])
                          tk_["f"]()
                          act(dq[qi][:], dq[qi][:], AF.Silu, [f"dq{qi}"], [f"dq{qi}"])
                          if qi < 2:
                              act(tb[7][:64, :], dq[qi][:], AF.Square, [f"dq{qi}"], ["tb7"])
                              mm(pss[:64, :], ones[:64, :64], tb[7][:64, :], ["ones", "tb7"], ["pss"])
                              act(rstd[:64, :], pss[:64, :], AF.Sqrt, ["pss"], ["rstd"], bias=EPS, scale=1.0)
                              add("dve", lambda e: e.reciprocal(out=rstd[:64, :], in_=rstd[:64, :]), ["rstd"], ["rstd"])
                              stt("dve", dq[qi][:], dq[qi][:], 0.125 if qi == 0 else 1.0, rstd[:64, :], ALU.mult, ALU.mult,
                                  [f"dq{qi}", "rstd"], [f"dq{qi}"])
                      cp("pool", qbf[:], dq[0][:], ["dq0"], ["qbf"])
                      cp("pool", kbf[:], dq[1][:], ["dq1"], ["kbf"])
                      tt("dve", tf[5][:64, :], dq[1][:], tf[4][:64, :], ALU.mult, ["dq1", "tf4"], ["tf5"])
                      cp("pool", kbb[:], tf[5][:64, :], ["tf5"], ["kbb"])
                      act(tf[6][:64, :], tf[3][:64, :], AF.Exp, ["tf3"], ["tf6"])
                      tt("dve", q2b[:], dq[0][:], tf[6][:64, :], ALU.mult, ["dq0", "tf6"], ["q2b"])
                      tt("dve", tb[3][:64, :], tf[5][:64, :], tf[6][:64, :], ALU.mult, ["tf5", "tf6"], ["tb3"])
                      tt("dve", tb[4][:64, :], dq[2][:], tf[4][:64, :], ALU.mult, ["dq2", "tf4"], ["tb4"])
                      cp("dve", blast[:, :], bB3[:, :, 63], ["tf3"], ["blast"])
                      act(elast[:], blast[:], AF.Exp, ["blast"], ["elast"])
                      tt("dve", tf[6][:64, :].rearrange("p (c t) -> p c t", t=64), blast[:].unsqueeze(2).to_broadcast([64, NCH, 64]),
                         bB3, ALU.subtract, ["blast", "tf3"], ["tf6"])
                      act(tf[6][:64, :], tf[6][:64, :], AF.Exp, ["tf6"], ["tf6"])
                      tt("dve", tb[5][:64, :], dq[1][:], tf[6][:64, :], ALU.mult, ["dq1", "tf6"], ["tb5"])
                      tk_["f"]()
                      for src_t, skey_t, dst_t, dkey_t, pi in ((tb[3], "tb3", kbetok, "kbetok", 0), (tb[4], "tb4", vbtok, "vbtok", 1),
                                                               (tb[5], "tb5", k2tok, "k2tok", 0)):
                          for c in range(NCH):
                              tr(ptb[pi][:64, c * 64:(c + 1) * 64], src_t[:64, c * 64:(c + 1) * 64], ident[:64, :64],
                                 [skey_t, "ident"], [f"ptb{pi}"])
                          cp("act", dst_t[:], ptb[pi][:64, :NCH * 64].rearrange("p (c d) -> p c d", d=64), [f"ptb{pi}"], [dkey_t])
                          tk_["f"]()
                      for c in range(NCH):
                          cs = slice(c * 64, (c + 1) * 64)
                          mm(pq[0][:64, cs], kbb[:, cs], kbf[:, cs], ["kbb", "kbf"], ["pq0"])
                          mm(pq[1][:64, cs], kbf[:, cs], kbb[:, cs], ["kbb", "kbf"], ["pq1"])
                          mm(pq[2][:64, cs], kbf[:, cs], qbf[:, cs], ["kbf", "qbf"], ["pq2"])
                      v3 = lambda p_: p_[:64, :].rearrange("p (c t) -> p c t", t=64)
                      tt("dve", tf[5][:64, :].rearrange("p (c t) -> p c t", t=64), lm[:], mLs[:], ALU.mult, ["lm", "mLs"], ["tf5"])
                      tt("dve", Zb[0][:], v3(pq[0]), tf[5][:64, :].rearrange("p (c t) -> p c t", t=64), ALU.mult, ["pq0", "tf5"], ["Zb0"])
                      tt("dve", tf[5][:64, :].rearrange("p (c t) -> p c t", t=64), lmT[:], mUs[:], ALU.mult, ["lmT", "mUs"], ["tf5"])
                      tt("dve", Yb[0][:], v3(pq[1]), tf[5][:64, :].rearrange("p (c t) -> p c t", t=64), ALU.mult, ["pq1", "tf5"], ["Yb0"])
                      tt("dve", tf[5][:64, :].rearrange("p (c t) -> p c t", t=64), lmT[:], mU[:], ALU.mult, ["lmT", "mU"], ["tf5"])
                      tt("dve", aqk[:], v3(pq[2]), tf[5][:64, :].rearrange("p (c t) -> p c t", t=64), ALU.mult, ["pq2", "tf5"], ["aqk"])
                      tt("dve", P32[:], Yb[0][:], identf[:64, :64].unsqueeze(1).to_broadcast([64, NCH, 64]), ALU.add,
                         ["Yb0", "identf"], ["P32"])
                      cp("pool", Pb[:], P32[:], ["P32"], ["Pb"])
                      tk_["f"]()
                      for lv in range(5):
                          a_, b_i = lv % 2, (lv + 1) % 2
                          for c in range(NCH):
                              cs = slice(c * 64, (c + 1) * 64)
                              mm(pq[0][:64, cs], Yb[a_][:, c, :], Zb[a_][:, c, :], [f"Yb{a_}", f"Zb{a_}"], ["pq0"])
                          cp("act", Zb[b_i][:], v3(pq[0]), ["pq0"], [f"Zb{b_i}"])
                          if lv < 4:
                              for c in range(NCH):
                                  cs = slice(c * 64, (c + 1) * 64)
                                  mm(pq[1][:64, cs], Zb[a_][:, c, :], Yb[a_][:, c, :], [f"Yb{a_}", f"Zb{a_}"], ["pq1"])
                              cp("act", Yb[b_i][:], v3(pq[1]), ["pq1"], [f"Yb{b_i}"])
                          for c in range(NCH):
                              cs = slice(c * 64, (c + 1) * 64)
                              mm(pq[2][:64, cs], Zb[b_i][:, c, :], Pb[:, c, :], [f"Zb{b_i}", "Pb"], ["pq2"])
                          tt("dve", Pb[:], P32[:], v3(pq[2]), ALU.add, ["P32", "pq2"], ["Pb"])
                          tt("dve", P32[:], P32[:], v3(pq[2]), ALU.add, ["P32", "pq2"], ["P32"])
                          tk_["f"]()
                      for c in range(NCH):
                          cs = slice(c * 64, (c + 1) * 64)
                          mm(pq[0][:64, cs], Pb[:, c, :], vbtok[:, c, :], ["Pb", "vbtok"], ["pq0"])
                          mm(pq[1][:64, cs], kbetok[:, c, :], Pb[:, c, :], ["Pb", "kbetok"], ["pq1"])
                      cp("act", ud[:], v3(pq[0]), ["pq0"], ["ud"])
                      cp("act", wTd[:], v3(pq[1]), ["pq1"], ["wTd"])
                      tk_["f"]()
                      S32 = Sd32[h]
                      Sb_ = Sdbf[h]
                      for c in range(NCH):
                          cs = slice(c * 64, (c + 1) * 64)
                          cp("dve", Sb_[:], S32[:], [S32.name], [Sb_.name])
                          mm(pq[0][:64, 0:64], wTd[:, c, :], Sb_[:], ["wTd", Sb_.name], ["pq0"])
                          tt("dve", vnew[:], ud[:, c, :], pq[0][:64, 0:64], ALU.subtract, ["ud", "pq0"], ["vnew"])
                          mm(pq[3][:64, cs], Sb_[:], q2b[:, cs], [Sb_.name, "q2b"], ["pq3"], start=True, stop=False)
                          mm(pq[3][:64, cs], vnew[:], aqk[:, c, :], ["vnew", "aqk"], ["pq3"], start=False, stop=True)
                          mm(pq[2][:64, 0:64], k2tok[:, c, :], vnew[:], ["k2tok", "vnew"], ["pq2"])
                          stt("dve", S32[:], S32[:], elast[:, c:c + 1], pq[2][:64, 0:64], ALU.mult, ALU.add,
                              [S32.name, "elast", "pq2"], [S32.name])
                          tk_["f"]()
                      lin_fm(f"D2_{l}", 8 + h * 64, 8 + h * 64 + 64, hbr, 8, pd[0][:64, :], "pd0", ["hb"])
                      act(tf[0][:64, :], pd[0][:64, :], AF.Silu, ["pd0"], ["tf0"])
                      head_out(pq[3][:64, :], tf[0][:64, :], ["tf0"], 42, l, 12 + h)

                  for _ in att:
                      pass
                  tk_["f"] = lambda: None
                  ck("D", l, n, tsl, dst, dkey)
                  for j in range(4):
                      need(f"O{j}_{l}")
                      wvw, wk = wview(f"O{j}_{l}")
                      for cc in range(2):
                          oc = j * 2 + cc
                          for k in range(16):
                              mm(pd[oc % 2][:, :], wvw[:, k, cc * 128:(cc + 1) * 128], mix[:, k, :], [wk, "mix"], [f"pd{oc % 2}"],
                                 start=(k == 0), stop=(k == 15))
                          cp("act", yb[:, oc, :], pd[oc % 2][:, :], [f"pd{oc % 2}"], ["yb"])
                  post_norm_residual(8, l)

                  ck("O", l, n, tsl, dst, dkey)
                  rmsnorm_x(16, l)
                  for c in range(NFF):
                      j, cc = c // 4, c % 4
                      need(f"G{j}_{l}", f"U{j}_{l}")
                      pg, kg, pu, ku = ((pd[0], "pd0", pd[1], "pd1"), (psa[0], "psa0", psa[1], "psa1"))[c % 2]
                      gr, grk = ((graw, "graw"), (graw2, "graw2"))[c % 2]
                      ta, tak, tb_, tbk = ((tf[0], "tf0", tf[1], "tf1"), (tf[2], "tf2", tf[3], "tf3"))[c % 2]
                      lin_fm(f"G{j}_{l}", cc * 128, cc * 128 + 128, hbr, 8, pg[:, :], kg, ["hb"])
                      lin_fm(f"U{j}_{l}", cc * 128, cc * 128 + 128, hbr, 8, pu[:, :], ku, ["hb"])
                      cp("pool", gr[:, 0:2], gtail[:, c, :], ["gtail"], [grk])
                      cp("act", gr[:, 2:TT + 2], pg[:, :], [kg], [grk])
                      cw = lambda jj: pp[:, l, 100 + c * 3 + jj:101 + c * 3 + jj]
                      ts("dve", ta[:], gr[:, 0:TT], cw(0), ALU.mult, [grk, "pp"], [tak])
                      stt("dve", ta[:], gr[:, 1:TT + 1], cw(1), ta[:], ALU.mult, ALU.add, [grk, "pp", tak], [tak])
                      stt("dve", ta[:], gr[:, 2:TT + 2], cw(2), ta[:], ALU.mult, ALU.add, [grk, "pp", tak], [tak])
                      cp("pool", gtail[:, c, :], gr[:, TT:TT + 2], [grk], ["gtail"])
                      act(tb_[:], ta[:], AF.Gelu_apprx_tanh, [tak], [tbk])
                      tt("dve", hid[:, c, :], tb_[:], pu[:, :], ALU.mult, [tbk, ku], ["hid"])
                  for j in range(8):
                      need(f"W{j}_{l}")
                      wvw, wk = wview(f"W{j}_{l}")
                      for k in range(NFF):
                          mm(pd[j % 2][:, :], wvw[:, k, :], hid[:, k, :], [wk, "hid"], [f"pd{j % 2}"], start=(k == 0), stop=(k == NFF - 1))
                      cp("act", yb[:, j, :], pd[j % 2][:, :], [f"pd{j % 2}"], ["yb"])
                  post_norm_residual(24, l)
                  dma("pool", dst.rearrange("(c p) t -> p c t", p=128)[:, :, tsl], xt[:], ["xt"], [(dkey, n)], "dout")

        except _Stop:
            pass
        final = [k for k in Sd.last_w if isinstance(k, tuple) and k[0] in ("outT", "xs0", "xs1")]
        Sd.emit(final_reads=final)
    return nc


def _pack_params(inp, DEPTH):
    pp = np.zeros((128, DEPTH, NPP), np.float32)
    for l in range(DEPTH):
        for i, nm in enumerate(("pre_mix_g", "post_mix_g", "pre_ffn_g", "post_ffn_g")):
            pp[:, l, i * 8:(i + 1) * 8] = np.asarray(inp[nm][l]).reshape(8, 128).T
        pp[:64, l, 32:36] = np.asarray(inp["hgrn_lb_logits"][0]).reshape(4, 64).T
        if DEPTH > 1:
            pp[:64, l, 36:40] = np.asarray(inp["hgrn_lb_logits"][1]).reshape(4, 64).T
        pp[:64, l, 40] = inp["hgrn_norm_g"][l]
        pp[:64, l, 41] = inp["gla_norm_g"][l]
        pp[:64, l, 42] = inp["gdn_norm_g"][l]
        pp[:32, l, 43:47] = np.asarray(inp["gla_b_gk"][l]).reshape(4, 32).T
        pp[:, l, 47:49] = np.asarray(inp["mla_q_norm_g"][l]).reshape(2, 128).T
        pp[:, l, 49] = inp["mla_kv_norm_g"][l]
        cw = np.asarray(inp["gdn_conv_w"][l])
        pp[:64, l, 50:98] = cw.reshape(4, 12, 64).transpose(2, 1, 0).reshape(64, 48)
        pp[:4, l, 98] = inp["gdn_a_log"][l]
        pp[:4, l, 99] = inp["gdn_dt_bias"][l]
        fw = np.asarray(inp["ffn_conv_w"][l])
        pp[:, l, 100:166] = fw.reshape(3, NFF, 128).transpose(2, 1, 0).reshape(128, 66)
    return pp


def _rope_tables(S):
    inv = (np.float32(10000.0) ** (-np.arange(0, 32, 2, dtype=np.float32) / np.float32(32))).astype(np.float32)
    ang = (np.arange(S, dtype=np.float32)[:, None] * inv[None, :]).astype(np.float32)
    cos, sin = np.cos(ang).astype(np.float32).T, np.sin(ang).astype(np.float32).T
    tabk = np.zeros((32, 2, S), np.float32)
    tabk[0:16, 0], tabk[16:32, 0] = cos, cos
    tabk[0:16, 1], tabk[16:32, 1] = sin, sin
    tabq = np.zeros((96, 2, S), np.float32)
    tabq[0:64, 0] = 1.0
    tabq[64:96] = tabk
    return tabq, tabk


_CACHE = {}


def run(inp, n_cores=None):
    x = np.asarray(inp["x"], np.float32)
    B, S, _ = x.shape
    DEPTH = int(np.asarray(inp["w_in"]).shape[0])
    key = (S, DEPTH)
    if key not in _CACHE:
        _CACHE[key] = build(S, DEPTH)
    nc = _CACHE[key]
    pp = _pack_params(inp, DEPTH)
    tabq, tabk = _rope_tables(S)
    shared = {k: np.ascontiguousarray(np.asarray(inp[k], np.float32)) for k in
              ("w_in", "w_out", "ffn_w_gate", "ffn_w_up", "ffn_w_down", "mla_w_uq", "mla_w_ukv", "gla_w_gk2")}
    shared.update(pp=pp, tabq=tabq, tabk=tabk)
    in_maps = []
    for b in range(B):
        m = dict(shared)
        m["xT"] = np.ascontiguousarray(x[b].T)
        in_maps.append(m)
    res = run_bass_kernel_spmd(nc, in_maps, core_ids=list(range(B)))
    out = np.stack([np.ascontiguousarray(r["outT"].T) for r in res.results], axis=0)
    return out.astype(np.float32)


def kernel(**inputs):
    return run(inputs)
```
